# Optimizing a Trainium2 kernel written in Bass

```python
import jax, jax.numpy as jnp
from jax import lax
import numpy as np

D_MODEL = 1024
BATCH = 16
SEQ = 2048
DEPTH = 1

MIX_WIDTH = D_MODEL
HEAD_DIM = 64
A_WIDTH = MIX_WIDTH // 2
B_WIDTH = MIX_WIDTH - A_WIDTH
A_GROUPS = A_WIDTH // HEAD_DIM
A_GROUP_DIM = A_WIDTH // A_GROUPS
B_HEADS = B_WIDTH // HEAD_DIM
CHUNK = 128
Q_BLOCK = 128
D_FF = ((8 * D_MODEL // 3 + 255) // 256) * 256
COL_A_U = 0
COL_A_V = COL_A_U + A_WIDTH
COL_B_Q = COL_A_V + A_WIDTH
COL_B_K = COL_B_Q + B_WIDTH
COL_B_V = COL_B_K + B_WIDTH
COL_B_F = COL_B_V + B_WIDTH
IN_COLS = COL_B_F + B_HEADS
RMS_EPS = 1e-6
LN_EPS = 1e-5

kernel_name = "hymba_gmlp_fox_hybrid_block"


def _rmsnorm(x, g):
    x32 = x.astype(jnp.float32)
    y = x32 * lax.rsqrt(jnp.mean(x32 * x32, axis=-1, keepdims=True) + RMS_EPS)
    return (y * g.astype(jnp.float32)).astype(x.dtype)


def _layernorm(x, g, b):
    x32 = x.astype(jnp.float32)
    mu = jnp.mean(x32, axis=-1, keepdims=True)
    var = jnp.mean(jnp.square(x32 - mu), axis=-1, keepdims=True)
    y = (x32 - mu) * lax.rsqrt(var + LN_EPS)
    return (y * g.astype(jnp.float32) + b.astype(jnp.float32)).astype(x.dtype)


def _spatial_gating(u, v, ln_g, ln_b, w_s, b_s):
    B, S, _ = v.shape
    nc = S // CHUNK
    v = _layernorm(v, ln_g, ln_b)
    v = v.reshape(B, nc, CHUNK, A_GROUPS, A_GROUP_DIM)
    tril = jnp.tril(jnp.ones((CHUNK, CHUNK), dtype=w_s.dtype))
    w_masked = w_s * tril[None]
    mixed = jnp.einsum('gts,bcsgd->bctgd', w_masked, v) + b_s.T[None, None, :, :, None]
    out = u.reshape(B, nc, CHUNK, A_GROUPS, A_GROUP_DIM) * mixed
    return out.reshape(B, S, A_WIDTH)


def _forgetting_attention(q, k, v, f_logit, b_f):
    B, S, _ = q.shape
    def heads(t):
        return t.reshape(B, S, B_HEADS, HEAD_DIM).transpose(0, 2, 1, 3)
    q, k, v = heads(q), heads(k), heads(v)
    log_f = jax.nn.log_sigmoid(f_logit.astype(jnp.float32) + b_f.astype(jnp.float32))
    c = jnp.cumsum(log_f, axis=1).transpose(0, 2, 1)
    scale = HEAD_DIM ** -0.5
    pos = jnp.arange(S)
    outs = []
    for i in range(S // Q_BLOCK):
        q0, q1 = i * Q_BLOCK, (i + 1) * Q_BLOCK
        q_blk = q[:, :, q0:q1]
        k_blk = k[:, :, :q1]
        v_blk = v[:, :, :q1]
        logits = jnp.einsum('bhqd,bhkd->bhqk', q_blk, k_blk).astype(jnp.float32) * scale
        logits = logits + c[:, :, q0:q1, None] - c[:, :, None, :q1]
        mask = pos[q0:q1, None] >= pos[None, :q1]
        logits = jnp.where(mask[None, None], logits, -jnp.inf)
        p = jax.nn.softmax(logits, axis=-1).astype(v.dtype)
        outs.append(jnp.einsum('bhqk,bhkd->bhqd', p, v_blk))
    o = jnp.concatenate(outs, axis=2)
    return o.transpose(0, 2, 1, 3).reshape(B, S, B_WIDTH)


def setup_inputs(seed: int = 0) -> dict:
    key = jax.random.key(seed)
    ks = jax.random.split(key, 20)
    f32 = jnp.float32
    L = DEPTH
    def nrm(k, shape, scale):
        return jax.random.normal(k, shape, f32) * scale
    return {
        "x": jax.random.normal(ks[0], (BATCH, SEQ, D_MODEL), f32),
        "pre_mix_gain": 1.0 + nrm(ks[1], (L, D_MODEL), 0.02),
        "w_in": nrm(ks[2], (L, D_MODEL, IN_COLS), D_MODEL ** -0.5),
        "ln_v_gain": 1.0 + nrm(ks[3], (L, A_WIDTH), 0.02),
        "ln_v_bias": nrm(ks[4], (L, A_WIDTH), 0.02),
        "w_spatial": nrm(ks[5], (L, A_GROUPS, CHUNK, CHUNK), 0.5 * CHUNK ** -0.5),
        "b_spatial": 1.0 + nrm(ks[6], (L, A_GROUPS, CHUNK), 0.1),
        "b_forget": 2.0 + nrm(ks[7], (L, B_HEADS), 0.5),
        "out_norm_a_gain": 1.0 + nrm(ks[8], (L, A_WIDTH), 0.02),
        "out_norm_b_gain": 1.0 + nrm(ks[9], (L, B_WIDTH), 0.02),
        "w_out": nrm(ks[10], (L, MIX_WIDTH, D_MODEL), MIX_WIDTH ** -0.5),
        "post_mix_gain": 1.0 + nrm(ks[11], (L, D_MODEL), 0.02),
        "pre_ffn_gain": 1.0 + nrm(ks[12], (L, D_MODEL), 0.02),
        "w_ffn_in": nrm(ks[13], (L, D_MODEL, 2 * D_FF), D_MODEL ** -0.5),
        "w_ffn_out": nrm(ks[14], (L, D_FF, D_MODEL), D_FF ** -0.5),
        "post_ffn_gain": 1.0 + nrm(ks[15], (L, D_MODEL), 0.02),
    }


def reference(x, pre_mix_gain, w_in, ln_v_gain, ln_v_bias, w_spatial, b_spatial,
              b_forget, out_norm_a_gain, out_norm_b_gain, w_out, post_mix_gain,
              pre_ffn_gain, w_ffn_in, w_ffn_out, post_ffn_gain):
    h = x
    for l in range(DEPTH):
        n = _rmsnorm(h, pre_mix_gain[l])
        proj = jnp.einsum('bsd,dc->bsc', n, w_in[l])
        a_u = jax.nn.gelu(proj[..., COL_A_U:COL_A_V], approximate=False)
        a_v = jax.nn.gelu(proj[..., COL_A_V:COL_B_Q], approximate=False)
        a_out = _spatial_gating(a_u, a_v, ln_v_gain[l], ln_v_bias[l],
                                w_spatial[l], b_spatial[l])
        b_out = _forgetting_attention(proj[..., COL_B_Q:COL_B_K],
                                      proj[..., COL_B_K:COL_B_V],
                                      proj[..., COL_B_V:COL_B_F],
                                      proj[..., COL_B_F:IN_COLS],
                                      b_forget[l])
        merged = jnp.concatenate([_rmsnorm(a_out, out_norm_a_gain[l]),
                                  _rmsnorm(b_out, out_norm_b_gain[l])], axis=-1)
        mix = jnp.einsum('bsc,cd->bsd', merged, w_out[l])
        h = h + _rmsnorm(mix, post_mix_gain[l])
        n2 = _rmsnorm(h, pre_ffn_gain[l])
        gu = jnp.einsum('bsd,df->bsf', n2, w_ffn_in[l])
        ff = jax.nn.silu(gu[..., :D_FF]) * gu[..., D_FF:]
        ff = jnp.einsum('bsf,fd->bsd', ff, w_ffn_out[l])
        h = h + _rmsnorm(ff, post_ffn_gain[l])
    return h
```

```python
import numpy as np
import ml_dtypes
from contextlib import ExitStack
import concourse.bass as bass
import concourse.mybir as mybir
from concourse.bass_utils import run_bass_kernel_spmd

F32 = mybir.dt.float32
BF16 = mybir.dt.bfloat16
AF = mybir.ActivationFunctionType
ALU = mybir.AluOpType

D = 1024
SEQ = 2048
NCORES = 8
TOK = 4096
INC = 2568
DFF = 2816
NFT = 22
RMS_EPS = 1e-6
LN_EPS = 1e-5

G_POSTMIX = 0
G_POSTFFN = 1024
G_LNVG = 2048
G_LNVB = 2560
G_BS = 3072
G_BFOR = 3584
G_GT_PRE = 3592
G_GT_OUTN = 3600
G_GT_FFN = 3608
G_NEGH = 3616
GW = 3624

NSLOT = 4
import os as _os0
SAME_ENG_SYNC = int(_os0.environ.get("SAME_ENG_SYNC", "2"))
SLOT_ELEMS = 4096


class Chan:
    def __init__(self, sem):
        self.sem = sem
        self.count = 0


class Op:
    __slots__ = ("eng", "fn", "deps", "chan", "ninc", "signal", "sigval", "idx", "hard")


class Prog:
    ENGS = ("pe", "act", "dve", "pool", "sp")

    def __init__(self):
        self.ops = []
        self.recs = {}

    def add(self, eng, fn, reads=(), writes=(), chan=None, ninc=1):
        op = Op()
        op.eng = eng
        op.fn = fn
        op.chan = chan
        op.ninc = ninc
        op.signal = False
        op.sigval = None
        op.idx = len(self.ops)
        reads = list(reads)
        writes = list(writes)
        psr = [r for r in reads if r[0] == "ps"]
        if psr:
            reads = [r for r in reads if r[0] != "ps"]
            writes = writes + psr
        writes = [(m, (lo // 2048) * 2048, -((-hi) // 2048) * 2048) if m == "ps" else (m, lo, hi) for (m, lo, hi) in writes]
        deps = {}
        hard = {}
        for (mem, lo, hi) in reads:
            for rec in self.recs.get(mem, ()):
                if rec[3] and rec[0] < hi and lo < rec[1]:
                    deps[rec[2].idx] = rec[2]
                    hard[rec[2].idx] = True
        for (mem, lo, hi) in writes:
            for rec in self.recs.get(mem, ()):
                if rec[0] < hi and lo < rec[1]:
                    deps[rec[2].idx] = rec[2]
                    if rec[3]:
                        hard[rec[2].idx] = True
        op.deps = list(deps.values())
        op.hard = hard
        for (mem, lo, hi) in writes:
            lst = self.recs.setdefault(mem, [])
            lst[:] = [r for r in lst if not (lo <= r[0] and r[1] <= hi)]
            lst.append((lo, hi, op, True))
        for (mem, lo, hi) in reads:
            lst = self.recs.setdefault(mem, [])
            lst[:] = [r for r in lst if not ((not r[3]) and r[0] == lo and r[1] == hi
                                             and r[2].eng == eng and r[2].chan is None and chan is None)]
            lst.append((lo, hi, op, False))
        self.ops.append(op)
        return op

    @staticmethod
    def needs_sync(op, d):
        if d.eng != op.eng or op.chan is not None:
            return True
        if op.eng == "pe":
            return False
        if SAME_ENG_SYNC == 2:
            return True
        return bool(SAME_ENG_SYNC) and op.hard.get(d.idx, False)

    def finalize(self, eng_sems):
        for op in self.ops:
            for d in op.deps:
                if d.chan is not None:
                    continue
                if self.needs_sync(op, d):
                    d.signal = True
        cnt = {e: 0 for e in self.ENGS}
        for op in self.ops:
            if op.chan is not None:
                op.chan.count += 16 * op.ninc
                op.sigval = op.chan.count
            elif op.signal:
                cnt[op.eng] += 1
                op.sigval = cnt[op.eng]
        self.eng_sems = eng_sems

    def run_engine(self, e, eng):
        seen = {}
        for op in self.ops:
            if op.eng != eng:
                continue
            need = {}
            for d in op.deps:
                if d.chan is not None:
                    key = id(d.chan)
                    sem = d.chan.sem
                elif self.needs_sync(op, d):
                    key = d.eng
                    sem = self.eng_sems[d.eng]
                else:
                    continue
                if key not in need or need[key][1] < d.sigval:
                    need[key] = (sem, d.sigval)
            for key, (sem, val) in need.items():
                if seen.get(key, 0) >= val:
                    continue
                seen[key] = val
                e.wait_ge(sem, val)
            if op.fn is None:
                continue
            ins = op.fn(e)
            if op.chan is not None:
                if not isinstance(ins, (list, tuple)):
                    ins = [ins]
                assert len(ins) == op.ninc
                for i in ins:
                    i.then_inc(op.chan.sem, 16)
            elif op.signal:
                ins.then_inc(self.eng_sems[eng], 1)


def build_nc(ngroups=8, debug=False, upto="G"):
    nc = bass.Bass("TRN2", target_bir_lowering=False)
    P = Prog()
    es = ExitStack()

    def dram_in(name, shape, dt):
        return nc.dram_tensor(name, shape, dt, kind="ExternalInput").ap()

    x_d = dram_in("x", [TOK, D], F32)
    w_in_d = dram_in("w_in", [D, INC], F32)
    w_out_d = dram_in("w_out", [D, D], F32)
    w_fi_d = dram_in("w_ffn_in", [D, 2 * DFF], F32)
    w_fo_d = dram_in("w_ffn_out", [DFF, D], F32)
    gains_d = dram_in("gains", [128, GW], F32)
    cf_d = dram_in("cf", [128, 512], F32)
    cb_d = dram_in("cb", [128, 256], BF16)
    wsT_d = dram_in("wsT", [128, 1024], F32)
    out_d = nc.dram_tensor("out", [TOK, D], F32, kind="ExternalOutput").ap()
    dbg_d = None
    if debug:
        dbg_d = nc.dram_tensor("dbg", [TOK, D], F32, kind="ExternalOutput").ap()
        dbg2_d = nc.dram_tensor("dbg2", [TOK, D], F32, kind="ExternalOutput").ap()

    s_win = nc.dram_tensor("s_win", [128, 8 * INC], BF16).ap()
    s_wout = nc.dram_tensor("s_wout", [128, 8 * D], BF16).ap()
    s_wfi = nc.dram_tensor("s_wfi", [128, 8 * 2 * DFF], BF16).ap()
    s_wfo = nc.dram_tensor("s_wfo", [128, NFT * D], BF16).ap()
    s_win3 = s_win.rearrange("p (k c) -> p k c", k=8)
    s_wout3 = s_wout.rearrange("p (k c) -> p k c", k=8)
    s_wfi4 = s_wfi.rearrange("p (k u c) -> p k u c", k=8, u=2)
    s_wfo3 = s_wfo.rearrange("p (k c) -> p k c", k=NFT)
    w_in3 = w_in_d.rearrange("(k p) c -> p k c", p=128)
    w_out3 = w_out_d.rearrange("(k p) c -> p k c", p=128)
    w_fi4 = w_fi_d.rearrange("(k p) (u c) -> p k u c", p=128, u=2)
    w_fo3 = w_fo_d.rearrange("(k p) c -> p k c", p=128)

    def sb(name, shape, dt):
        return es.enter_context(nc.sbuf_tensor("sb_" + name, shape, dt))

    def sem(name):
        return es.enter_context(nc.semaphore(name))

    def chan(name):
        return Chan(sem(name))

    gains = sb("gains", [128, GW], F32)
    cf = sb("cf", [128, 512], F32)
    cb = sb("cb", [128, 256], BF16)
    wsT_b = sb("wsT_b", [128, 1024], BF16)
    win_f = sb("win_f", [128, 64], BF16)
    KT = sb("KT", [128, 8 * SEQ], BF16)
    VA = sb("VA", [128, 16 * 8 * 65], BF16)
    hg = sb("hg", [128, 2 * 4 * D], F32)
    actT = sb("actT", [128, 8 * 512], BF16)
    xn = sb("xn", [128, 2 * D], BF16)
    junk = sb("junk", [128, D], BF16)
    ug = sb("ug", [128, 4 * 512], F32)
    qaug = sb("qaug", [128, 4 * 8 * 70], BF16)
    kaug = sb("kaug", [128, 4 * 8 * 70], BF16)
    arena = sb("arena", [128, NFT * 512], BF16)
    wsT_f = arena[:, 0:2048].bitcast(F32)
    ET = sb("ET", [128, 4 * 512], BF16)
    merged = sb("merged", [128, 4 * D], BF16)
    aout = sb("aout", [128, 512], F32)
    mt = sb("mt", [128, 2 * 512], F32)
    vt = mt
    sg = mt
    ring = sb("ring", [128, NSLOT * SLOT_ELEMS], BF16)
    st = sb("st", [128, 320], F32)
    Rsum = sb("Rsum", [128, 8], F32)
    ps = es.enter_context(nc.psum_tensor("ps", [128, 8, 512], F32))

    ident = cb[:, 0:128]
    maskneg = cb[:, 128:256]
    mask01 = cf[:, 0:128]
    trineg = cf[:, 128:256]
    onesneg = cf[:, 256:384]
    identf = cf[:, 384:512]

    KT3 = KT[:, :].rearrange("p (h t) -> p h t", h=8)
    VA4 = VA[:, :].rearrange("p (b h c) -> p b h c", b=16, h=8)
    hg4 = hg[:, :].rearrange("p (a b d) -> p a b d", a=2, b=4)
    actT3 = actT[:, :].rearrange("p (k t) -> p k t", k=8)
    ug3 = ug[:, :].rearrange("p (b d) -> p b d", b=4)
    qaug4 = qaug[:, :].rearrange("p (b h c) -> p b h c", b=4, h=8)
    kaug4 = kaug[:, :].rearrange("p (b h c) -> p b h c", b=4, h=8)
    QT3 = arena[:, 0:4096].rearrange("p (h t) -> p h t", h=8)
    bo3 = arena[:, 4096:8192].bitcast(F32).rearrange("p (b d) -> p b d", b=4)
    vln3 = arena[:, 8192:10240].rearrange("p (b d) -> p b d", b=4)
    ffT3 = arena[:, :].rearrange("p (f t) -> p f t", f=NFT)
    merged3 = merged[:, :].rearrange("p (b d) -> p b d", b=4)
    vt4 = merged[:, :].bitcast(F32).rearrange("p (b d) -> p b d", b=4)

    def R(mem, lo, hi):
        return (mem, lo, hi)

    def r_hg(hp, tb, lo=0, hi=D):
        return R("hg", ((hp * 4 + tb) * D + lo) * 4, ((hp * 4 + tb) * D + hi) * 4)

    def r_actT_tb(tb):
        return [R("actT", (k * 512 + tb * 128) * 2, (k * 512 + tb * 128 + 128) * 2) for k in range(8)]

    def r_actT_all():
        return [R("actT", 0, 8192)]

    def r_ps(bank, lo=0, hi=2048):
        return R("ps", bank * 2048 + lo, bank * 2048 + hi)

    def r_QT(h=None):
        if h is None:
            return R("arena", 0, 8192)
        return R("arena", h * 1024, (h + 1) * 1024)

    def r_bo(tb, lo=0, hi=512):
        return R("arena", 8192 + (tb * 512 + lo) * 4, 8192 + (tb * 512 + hi) * 4)

    def r_vln(tb):
        return R("arena", 16384 + tb * 1024, 16384 + (tb + 1) * 1024)

    def r_ffT(f):
        return R("arena", f * 1024, (f + 1) * 1024)

    def r_ring(s):
        return R("ring", s * SLOT_ELEMS * 2, (s + 1) * SLOT_ELEMS * 2)

    def r_st(c, n=1):
        return R("st", c * 4, (c + n) * 4)

    def r_KT(h, j):
        return R("KT", (h * SEQ + j * 128) * 2, (h * SEQ + (j + 1) * 128) * 2)

    def r_VA(j, h=None):
        if h is None:
            return R("VA", j * 8 * 65 * 2, (j + 1) * 8 * 65 * 2)
        return R("VA", (j * 8 + h) * 65 * 2, (j * 8 + h + 1) * 65 * 2)

    ST_SSX, ST_RSX = 0, 4
    ST_BN = 8
    ST_MV = 32
    ST_RSV = 40
    ST_SSA, ST_RSA = 44, 48
    ST_SSB, ST_RSB = 52, 56
    ST_SSM, ST_RSM = 60, 68
    ST_SSH, ST_RSH = 72, 76
    ST_SSO, ST_RSO = 80, 88
    ST_Z = 96
    ST_A = 128
    ST_SP = 160
    ST_C = 192
    ST_R1 = 256
    ST_RDEN = 232
    ST_TMP = 240

    def stc(c, n=1):
        return st[:, c:c + n]

    negh = gains[:, G_NEGH:G_NEGH + 1]

    ch_const = [chan("c_g"), chan("c_cf"), chan("c_cb"), chan("c_ws"), chan("c_wf")]
    ch_ring = [chan(f"c_ring{i}") for i in range(NSLOT)]
    ch_x = [chan(f"c_x{i}") for i in range(8)]
    ch_o = [chan(f"c_o{i}") for i in range(8)]
    ch_dbg = [chan(f"c_d{i}") for i in range(4)] if debug else None
    ch_dbg2 = [chan(f"c_e{i}") for i in range(2)] if debug else None

    P.add("sp", lambda e: e.dma_start(out=cb[:, :], in_=cb_d), writes=[R("cb", 0, 512)], chan=ch_const[2])
    P.add("sp", lambda e: e.dma_start(out=gains[:, G_BFOR:GW], in_=gains_d[:, G_BFOR:GW]),
          writes=[R("gains", G_BFOR * 4, GW * 4)], chan=ch_const[0])
    ch_gbig = chan("c_gbig")

    def late_consts():
        P.add("sp", lambda e: e.dma_start(out=gains[:, 0:G_BFOR], in_=gains_d[:, 0:G_BFOR]),
              writes=[R("gains", 0, G_BFOR * 4)], chan=ch_gbig)
        P.add("sp", lambda e: e.dma_start(out=cf[:, :], in_=cf_d), writes=[R("cf", 0, 2048)], chan=ch_const[1])
        P.add("sp", lambda e: e.dma_start(out=wsT_f[:, :], in_=wsT_d), writes=[R("arena", 0, 4096)], chan=ch_const[3])

    conv = []

    def add_conv(name, mem, lo, hi, out_ap, in_ap, split_u=False):
        c = chan("cv_" + name)
        if split_u:
            P.add("pool", lambda e: [e.dma_start(out=out_ap[:, :, u, :], in_=in_ap[:, :, u, :]) for u in range(2)],
                  writes=[R(mem, lo, hi)], chan=c, ninc=2)
        else:
            P.add("pool", lambda e: e.dma_start(out=out_ap, in_=in_ap), writes=[R(mem, lo, hi)], chan=c)

    def conv_win(c):
        add_conv(f"win{c}", "s_win", c * 512, (c + 1) * 512,
                 s_win3[:, :, c * 512:(c + 1) * 512], w_in3[:, :, c * 512:(c + 1) * 512])

    def conv_win_rest():
        for c in range(1, 5):
            conv_win(c)
        P.add("pool", lambda e: e.dma_start(out=win_f[:, :].rearrange("p (k c) -> p k c", k=8),
                                            in_=w_in3[:, :, 2560:2568]),
              writes=[R("win_f", 0, 128)], chan=ch_const[4])
    conv_win(0)
    conv_q = []
    for c in range(2):
        conv_q.append((f"wout{c}", "s_wout", c * 512, (c + 1) * 512,
                       s_wout3[:, :, c * 512:(c + 1) * 512], w_out3[:, :, c * 512:(c + 1) * 512], False))
    for s in range(11):
        conv_q.append((f"wfi{s}", "s_wfi", s * 256, (s + 1) * 256,
                       s_wfi4[:, :, :, s * 256:(s + 1) * 256], w_fi4[:, :, :, s * 256:(s + 1) * 256], True))
    for s in range(6):
        f0, f1 = s * 4, min(NFT, s * 4 + 4)
        conv_q.append((f"wfo{s}", "s_wfo", f0, f1, s_wfo3[:, f0:f1, :], w_fo3[:, f0:f1, :], False))

    def issue_convs(n):
        for _ in range(n):
            if conv_q:
                nm, mem, lo, hi, oap, iap, su = conv_q.pop(0)
                add_conv(nm, mem, lo, hi, oap, iap, split_u=su)

    def late_setup():
        P.add("dve", lambda e: e.tensor_tensor(
            out=wsT_b[:, :].rearrange("p (g t) -> p g t", g=8),
            in0=wsT_f[:, :].rearrange("p (g t) -> p g t", g=8),
            in1=mask01.unsqueeze(1).broadcast_to([128, 8, 128]), op=ALU.mult),
            reads=[R("arena", 0, 4096), R("cf", 0, 512)], writes=[R("wsT_b", 0, 2048)])
        P.add("pool", lambda e: e.memset(VA4[:, :, :, 64:65], 1.0), writes=[R("VA", 0, 16 * 8 * 65 * 2)])


    ring_state = {"next": 0}

    def load_slot(src_ap, view_fn, src_regions, split_u=False):
        s = ring_state["next"] % NSLOT
        ring_state["next"] += 1
        base = ring[:, s * SLOT_ELEMS:(s + 1) * SLOT_ELEMS]
        dst = view_fn(base)
        if split_u:
            P.add("sp", lambda e: [e.dma_start(out=dst[:, :, u, :], in_=src_ap[:, :, u, :]) for u in range(2)],
                  reads=src_regions, writes=[r_ring(s)], chan=ch_ring[s], ninc=2)
        else:
            P.add("sp", lambda e: e.dma_start(out=dst, in_=src_ap), reads=src_regions,
                  writes=[r_ring(s)], chan=ch_ring[s])
        return s, base

    def rms_rstd(ss_col, n, rs_col, inv_n, eps):
        P.add("pool", lambda e: e.tensor_scalar(out=stc(ST_TMP, n), in0=stc(ss_col, n), scalar1=inv_n, scalar2=eps,
                                                op0=ALU.mult, op1=ALU.add),
              reads=[r_st(ss_col, n)], writes=[r_st(ST_TMP, n)])
        P.add("pool", lambda e: e.tensor_tensor(out=stc(rs_col, n), in0=stc(ST_TMP, n),
                                                in1=negh.broadcast_to([128, n]) if n > 1 else negh, op=ALU.pow),
              reads=[r_st(ST_TMP, n), R("gains", G_NEGH * 4, G_NEGH * 4 + 4)], writes=[r_st(rs_col, n)])

    def transpose_block(src_ap_fn, src_regions, tb, tbank, gcol, dst_regions, evac_hook=None):
        psb = ps[:, tbank, :].bitcast(BF16).rearrange("p (k t) -> p k t", k=8)

        def pe_fn(e):
            ins = None
            for k in range(8):
                ins = e.transpose(out=psb[:, k, :], in_=src_ap_fn(k), identity=ident)
            return ins
        P.add("pe", pe_fn, reads=src_regions + [R("cb", 0, 256)], writes=[r_ps(tbank)])
        if evac_hook is not None:
            evac_hook()
        gT = gains[:, gcol:gcol + 8]
        P.add("dve", lambda e: e.tensor_tensor(
            out=actT3[:, :, tb * 128:(tb + 1) * 128], in0=psb,
            in1=gT.unsqueeze(2).broadcast_to([128, 8, 128]), op=ALU.mult),
            reads=[r_ps(tbank), R("gains", gcol * 4, gcol * 4 + 32)], writes=dst_regions)

    def xload(g):
        hp = g % 2
        for tb in range(4):
            r0 = g * 512 + tb * 128
            P.add("sp", lambda e, tb=tb, r0=r0, hp=hp: e.dma_start(out=hg4[:, hp, tb, :], in_=x_d[r0:r0 + 128, :]),
                  reads=[R("x", r0, r0 + 128)], writes=[r_hg(hp, tb)], chan=ch_x[hp * 4 + tb])

    def stageA(g, tbanks):
        hp = g % 2
        for tb in range(4):
            P.add("act", lambda e, tb=tb, hp=hp: e.activation(out=junk[:, :], in_=hg4[:, hp, tb, :], func=AF.Square,
                                                              accum_out=stc(ST_SSX + tb)),
                  reads=[r_hg(hp, tb)], writes=[R("junk", 0, 2048), r_st(ST_SSX + tb)])
        rms_rstd(ST_SSX, 4, ST_RSX, 1.0 / D, RMS_EPS)
        def emit_xn(tb):
            par = tb % 2
            P.add("dve", lambda e, tb=tb, par=par, hp=hp: e.tensor_scalar(
                out=xn[:, par * D:(par + 1) * D], in0=hg4[:, hp, tb, :], scalar1=stc(ST_RSX + tb), scalar2=None,
                op0=ALU.mult),
                reads=[r_hg(hp, tb), r_st(ST_RSX + tb)], writes=[R("xn", par * 2048, (par + 1) * 2048)])

        def emit_tr(tb):
            par = tb % 2
            transpose_block(lambda k, par=par: xn[:, par * D + k * 128: par * D + (k + 1) * 128],
                            [R("xn", par * 2048, (par + 1) * 2048)], tb, tbanks[tb], G_GT_PRE, r_actT_tb(tb),
                            evac_hook=hooks.get(tb))
        hooks = {0: lambda: emit_xn(2), 1: lambda: emit_xn(3)}
        emit_xn(0)
        emit_xn(1)
        emit_tr(0)
        emit_tr(1)
        emit_tr(2)
        emit_tr(3)

    def stagesBF(g):
        hp = g % 2
        seq = g // 4
        gi = g % 4
        row0 = g * 512
        blk0 = gi * 4

        if gi == 0:
            P.add("pool", lambda e: e.memset(Rsum[:, :], 0.0), writes=[R("Rsum", 0, 32)])

        P.add("pool", lambda e: e.memset(qaug4[:, :, :, 67:70], 1.0), writes=[R("qaug", 0, 4 * 8 * 70 * 2)])
        P.add("pool", lambda e: e.memset(kaug4[:, :, :, 64:67], 1.0), writes=[R("kaug", 0, 4 * 8 * 70 * 2)])
        pbank = [2, 3, 4]
        pcnt = [0]

        def inproj_tile(slot_ap, tb):
            b = pbank[pcnt[0] % 3]
            pcnt[0] += 1
            w3 = slot_ap.rearrange("p (k c) -> p k c", k=8)

            def pe_fn(e):
                ins = None
                for k in range(8):
                    ins = e.matmul(out=ps[:, b, :], lhsT=actT3[:, k, tb * 128:(tb + 1) * 128], rhs=w3[:, k, :],
                                   start=(k == 0), stop=(k == 7))
                return ins
            return b, pe_fn

        s, sl = load_slot(s_win3[:, :, 0:512], lambda a: a.rearrange("p (k c) -> p k c", k=8), [R("s_win", 0, 512)])
        for tb in range(4):
            b, pe_fn = inproj_tile(sl, tb)
            P.add("pe", pe_fn, reads=r_actT_tb(tb) + [r_ring(s)], writes=[r_ps(b)])
            P.add("act", lambda e, tb=tb, b=b: e.activation(out=ug3[:, tb, :], in_=ps[:, b, :], func=AF.Gelu),
                  reads=[r_ps(b)], writes=[R("ug", tb * 2048, (tb + 1) * 2048)])
        s, sl = load_slot(s_win3[:, :, 512:1024], lambda a: a.rearrange("p (k c) -> p k c", k=8), [R("s_win", 512, 1024)])
        vln_late = []
        for tb in range(4):
            b, pe_fn = inproj_tile(sl, tb)
            vtp = vt4[:, tb, :]
            rv = R("merged", tb * 2048, (tb + 1) * 2048)
            P.add("pe", pe_fn, reads=r_actT_tb(tb) + [r_ring(s)], writes=[r_ps(b)])
            P.add("act", lambda e, b=b, vtp=vtp: e.activation(out=vtp, in_=ps[:, b, :], func=AF.Gelu),
                  reads=[r_ps(b)], writes=[rv])
            P.add("dve", lambda e, tb=tb, vtp=vtp: e.bn_stats(out=stc(ST_BN + 6 * tb, 6), in_=vtp),
                  reads=[rv], writes=[r_st(ST_BN + 6 * tb, 6)])
            P.add("dve", lambda e, tb=tb: e.bn_aggr(out=stc(ST_MV + 2 * tb, 2), in_=stc(ST_BN + 6 * tb, 6)),
                  reads=[r_st(ST_BN + 6 * tb, 6)], writes=[r_st(ST_MV + 2 * tb, 2)])
            rms_rstd(ST_MV + 2 * tb + 1, 1, ST_RSV + tb, 1.0, LN_EPS)
            P.add("dve", lambda e, tb=tb, vtp=vtp: e.scalar_tensor_tensor(
                out=vtp, in0=vtp, scalar=stc(ST_MV + 2 * tb), in1=gains[:, G_LNVG:G_LNVG + 512],
                op0=ALU.subtract, op1=ALU.mult),
                reads=[rv, r_st(ST_MV + 2 * tb), R("gains", G_LNVG * 4, (G_LNVG + 512) * 4)], writes=[rv])
        if g == 0:
            issue_convs(6)
        FB = 5

        def pe_f(e):
            ins = None
            w3 = win_f[:, :].rearrange("p (k c) -> p k c", k=8)
            for tb in range(4):
                for k in range(8):
                    ins = e.matmul(out=ps[:, FB, tb * 8:(tb + 1) * 8], lhsT=actT3[:, k, tb * 128:(tb + 1) * 128],
                                   rhs=w3[:, k, :], start=(k == 0), stop=(k == 7), skip_group_check=True)
            return ins
        P.add("pe", pe_f, reads=r_actT_all() + [R("win_f", 0, 128)], writes=[r_ps(FB)])
        zz, aa, spp, ccc = stc(ST_Z, 32), stc(ST_A, 32), stc(ST_SP, 32), stc(ST_C, 32)
        P.add("dve", lambda e: e.scalar_tensor_tensor(
            out=zz.rearrange("p (t h) -> p t h", t=4), in0=ps[:, FB, 0:32].rearrange("p (t h) -> p t h", t=4), scalar=-1.0,
            in1=gains[:, G_BFOR:G_BFOR + 8].unsqueeze(1).broadcast_to([128, 4, 8]), op0=ALU.mult, op1=ALU.subtract),
            reads=[r_ps(FB), R("gains", G_BFOR * 4, G_BFOR * 4 + 32)], writes=[r_st(ST_Z, 32)])
        P.add("act", lambda e: e.activation(out=aa, in_=zz, func=AF.Abs), reads=[r_st(ST_Z, 32)], writes=[r_st(ST_A, 32)])
        P.add("act", lambda e: e.activation(out=aa, in_=aa, func=AF.Exp, scale=-1.0),
              reads=[r_st(ST_A, 32)], writes=[r_st(ST_A, 32)])
        P.add("act", lambda e: e.activation(out=aa, in_=aa, func=AF.Ln, bias=1.0),
              reads=[r_st(ST_A, 32)], writes=[r_st(ST_A, 32)])
        s, sl = load_slot(s_win3[:, :, 1024:1536], lambda a: a.rearrange("p (k c) -> p k c", k=8), [R("s_win", 1024, 1536)])
        for tb in range(4):
            b, pe_fn = inproj_tile(sl, tb)
            P.add("pe", pe_fn, reads=r_actT_tb(tb) + [r_ring(s)], writes=[r_ps(b)])
            P.add("dve", lambda e, tb=tb, b=b: e.tensor_scalar(
                out=qaug4[:, tb, :, 0:64], in0=ps[:, b, :].rearrange("p (h c) -> p h c", h=8),
                scalar1=0.125, scalar2=None, op0=ALU.mult),
                reads=[r_ps(b)], writes=[R("qaug", tb * 1120, (tb + 1) * 1120)])
        for tb in range(4):
            P.add("dve", lambda e, tb=tb: e.scalar_tensor_tensor(
                out=vln3[:, tb, :], in0=vt4[:, tb, :], scalar=stc(ST_RSV + tb), in1=gains[:, G_LNVB:G_LNVB + 512],
                op0=ALU.mult, op1=ALU.add),
                reads=[R("merged", tb * 2048, (tb + 1) * 2048), r_st(ST_RSV + tb), R("gains", G_LNVB * 4, (G_LNVB + 512) * 4)],
                writes=[r_vln(tb)])

        P.add("dve", lambda e: e.scalar_tensor_tensor(out=spp, in0=zz, scalar=0.0, in1=aa, op0=ALU.max, op1=ALU.add),
              reads=[r_st(ST_Z, 32), r_st(ST_A, 32)], writes=[r_st(ST_SP, 32)])
        CB = 7

        def pe_c(e):
            ins = None
            for tb in range(4):
                o = ps[:, CB, tb * 8:(tb + 1) * 8]
                e.matmul(out=o, lhsT=trineg, rhs=stc(ST_SP + 8 * tb, 8), start=True, stop=False, skip_group_check=True)
                for u in range(tb):
                    e.matmul(out=o, lhsT=onesneg, rhs=stc(ST_SP + 8 * u, 8), start=False, stop=False,
                             skip_group_check=True)
                ins = e.matmul(out=o, lhsT=onesneg, rhs=Rsum[:, :], start=False, stop=True, skip_group_check=True)
            return ins
        s, sl = load_slot(s_win3[:, :, 1536:2048], lambda a: a.rearrange("p (k c) -> p k c", k=8), [R("s_win", 1536, 2048)])
        for tb in range(4):
            b, pe_fn = inproj_tile(sl, tb)
            P.add("pe", pe_fn, reads=r_actT_tb(tb) + [r_ring(s)], writes=[r_ps(b)])
            P.add("act", lambda e, tb=tb, b=b: e.activation(
                out=kaug4[:, tb, :, 0:64], in_=ps[:, b, :].rearrange("p (h c) -> p h c", h=8), func=AF.Copy),
                reads=[r_ps(b)], writes=[R("kaug", tb * 1120, (tb + 1) * 1120)])
            if tb == 1:
                P.add("pe", pe_c, reads=[r_st(ST_SP, 32), R("Rsum", 0, 32), R("cf", 512, 1536)], writes=[r_ps(CB)])
        P.add("dve", lambda e: e.tensor_copy(out=ccc, in_=ps[:, CB, 0:32]), reads=[r_ps(CB)], writes=[r_st(ST_C, 32)])
        rq = R("qaug", 0, 4480)
        rk = R("kaug", 0, 4480)
        c3 = ccc.rearrange("p (t h) -> p t h", t=4)
        r13 = stc(ST_R1, 32).rearrange("p (t h) -> p t h", t=4)
        P.add("dve", lambda e: e.tensor_copy(out=qaug4[:, :, :, 64], in_=c3), reads=[r_st(ST_C, 32)], writes=[rq])
        P.add("dve", lambda e: e.tensor_tensor(out=r13, in0=c3, in1=qaug4[:, :, :, 64], op=ALU.subtract),
              reads=[r_st(ST_C, 32), rq], writes=[r_st(ST_R1, 32)])
        P.add("dve", lambda e: e.tensor_copy(out=qaug4[:, :, :, 65], in_=r13), reads=[r_st(ST_R1, 32)], writes=[rq])
        P.add("dve", lambda e: e.tensor_tensor(out=r13, in0=r13, in1=qaug4[:, :, :, 65], op=ALU.subtract),
              reads=[r_st(ST_R1, 32), rq], writes=[r_st(ST_R1, 32)])
        P.add("dve", lambda e: e.tensor_copy(out=qaug4[:, :, :, 66], in_=r13), reads=[r_st(ST_R1, 32)], writes=[rq])
        for tb in range(4):
            P.add("dve", lambda e, tb=tb: e.tensor_scalar(out=kaug4[:, tb, :, 67:70], in0=qaug4[:, tb, :, 64:67],
                                                          scalar1=-1.0, scalar2=None, op0=ALU.mult),
                  reads=[rq], writes=[rk])
        s, sl = load_slot(s_win3[:, :, 2048:2560], lambda a: a.rearrange("p (k c) -> p k c", k=8), [R("s_win", 2048, 2560)])
        for tb in range(4):
            b, pe_fn = inproj_tile(sl, tb)
            j = blk0 + tb
            P.add("pe", pe_fn, reads=r_actT_tb(tb) + [r_ring(s)], writes=[r_ps(b)])
            P.add("dve", lambda e, j=j, b=b: e.tensor_copy(
                out=VA4[:, j, :, 0:64], in_=ps[:, b, :].rearrange("p (h c) -> p h c", h=8)),
                reads=[r_ps(b)], writes=[r_VA(j)])
        gbanks = [2, 3, 4, 6]
        def emit_gmlp_pe():
            gbanks = [2, 3, 4, 6]
            for tb in range(4):
                gb = gbanks[tb]

                def pe_g(e, tb=tb, gb=gb):
                    ins = None
                    w3 = wsT_b[:, :].rearrange("p (g t) -> p g t", g=8)
                    for gg in range(8):
                        ins = e.matmul(out=ps[:, gb, gg * 64:(gg + 1) * 64], lhsT=w3[:, gg, :],
                                       rhs=vln3[:, tb, gg * 64:(gg + 1) * 64], start=True, stop=True, skip_group_check=True)
                    return ins
                P.add("pe", pe_g, reads=[R("wsT_b", 0, 2048), r_vln(tb)], writes=[r_ps(gb)])

        tbanks4 = [0, 1, 5, 7]
        tcnt = [0]
        for hp2 in range(4):
            for which in range(2):
                src4 = qaug4 if which == 0 else kaug4
                srcname = "qaug" if which == 0 else "kaug"
                if tcnt[0] == 4:
                    emit_gmlp_pe()
                    for tb_ in range(4):
                        P.add("dve", lambda e, tb=tb_, gb=gbanks[tb_]: e.tensor_tensor(
                            out=bo3[:, tb, :], in0=ps[:, gb, :], in1=gains[:, G_BS:G_BS + 512], op=ALU.add),
                            reads=[r_ps(gbanks[tb_]), R("gains", G_BS * 4, (G_BS + 512) * 4)], writes=[r_bo(tb_)])
                tbank = tbanks4[tcnt[0] % 4]
                tcnt[0] += 1
                pst = ps[0:70, tbank, :].bitcast(BF16).rearrange("p (h t) -> p h t", h=2)

                def pe_t(e, hp2=hp2, src4=src4, pst=pst):
                    ins = None
                    for hh in range(2):
                        for tb in range(4):
                            ins = e.transpose(out=pst[:, hh, tb * 128:(tb + 1) * 128], in_=src4[:, tb, 2 * hp2 + hh, :],
                                              identity=ident)
                    return ins
                P.add("pe", pe_t, reads=[R(srcname, 0, 4480), R("cb", 0, 256)], writes=[r_ps(tbank)])
                if which == 0:
                    P.add("act", lambda e, hp2=hp2, pst=pst: e.activation(out=QT3[0:70, 2 * hp2:2 * hp2 + 2, :], in_=pst,
                                                                          func=AF.Copy),
                          reads=[r_ps(tbank)], writes=[r_QT(2 * hp2), r_QT(2 * hp2 + 1)])
                else:
                    P.add("dve", lambda e, hp2=hp2, pst=pst: e.tensor_copy(
                        out=KT3[0:70, 2 * hp2:2 * hp2 + 2, blk0 * 128:(blk0 + 4) * 128], in_=pst),
                        reads=[r_ps(tbank)],
                        writes=[R("KT", (hh_ * SEQ + blk0 * 128) * 2, (hh_ * SEQ + (blk0 + 4) * 128) * 2)
                                for hh_ in (2 * hp2, 2 * hp2 + 1)])

        if g == 0:
            issue_convs(7)
        for tb in range(4):
            P.add("dve", lambda e, tb=tb: e.tensor_tensor(out=Rsum[:, :], in0=Rsum[:, :], in1=stc(ST_SP + 8 * tb, 8),
                                                          op=ALU.add),
                  reads=[R("Rsum", 0, 32), r_st(ST_SP + 8 * tb, 8)], writes=[R("Rsum", 0, 32)])

        for tb in range(4):
            P.add("dve", lambda e, tb=tb: e.tensor_tensor(out=ug3[:, tb, :], in0=bo3[:, tb, :], in1=ug3[:, tb, :],
                                                          op=ALU.mult),
                  reads=[r_bo(tb), R("ug", tb * 2048, (tb + 1) * 2048)], writes=[R("ug", tb * 2048, (tb + 1) * 2048)])
            if debug:
                r0 = row0 + tb * 128
                P.add("pool", lambda e, r0=r0, tb=tb: e.dma_start(out=dbg2_d[r0:r0 + 128, 0:512], in_=ug3[:, tb, :]),
                      reads=[R("ug", tb * 2048, (tb + 1) * 2048)], writes=[R("dbg2", r0, r0 + 128)], chan=ch_dbg2[0])

        def gmlp_piece(hh):
            if hh < 4:
                tb = hh
                P.add("dve", lambda e, tb=tb: e.scalar_tensor_tensor(
                    out=aout[:, :], in0=ug3[:, tb, :], scalar=1.0, in1=ug3[:, tb, :], op0=ALU.mult, op1=ALU.mult,
                    accum_out=stc(ST_SSA + tb)),
                    reads=[R("ug", tb * 2048, (tb + 1) * 2048)], writes=[R("aout", 0, 2048), r_st(ST_SSA + tb)])
                if hh == 3:
                    rms_rstd(ST_SSA, 4, ST_RSA, 1.0 / 512, RMS_EPS)
            else:
                tb = hh - 4
                P.add("dve", lambda e, tb=tb: e.tensor_scalar(out=merged3[:, tb, 0:512], in0=ug3[:, tb, :],
                                                              scalar1=stc(ST_RSA + tb), scalar2=None, op0=ALU.mult),
                      reads=[R("ug", tb * 2048, (tb + 1) * 2048), r_st(ST_RSA + tb)],
                      writes=[R("merged", tb * 2048, tb * 2048 + 1024)])

        nkb = blk0 + 4
        sbanks = [2, 3, 4]
        scnt = [0]
        ecnt = [0]
        TB7 = 7
        O3 = ps[:, TB7, 0:260].rearrange("p (q c) -> p q c", q=4)
        tails = []

        def make_tail(h, ob):
            otp = mt[0:65, (h % 2) * 512:(h % 2 + 1) * 512]
            r_ot = R("mt", (h % 2) * 2048, (h % 2 + 1) * 2048)

            def tail():
                def pe_tr(e):
                    ins = None
                    for tb in range(4):
                        ins = e.transpose(out=O3[:, tb, :], in_=otp[:, tb * 128:(tb + 1) * 128], identity=identf[0:65, 0:65])
                    return ins
                P.add("pe", pe_tr, reads=[r_ot, R("cf", 1536, 2048)], writes=[r_ps(TB7)])
                rdc = ST_RDEN + 4 * (h % 2)
                P.add("dve", lambda e: e.reciprocal(out=stc(rdc, 4), in_=O3[:, :, 64]),
                      reads=[r_ps(TB7)], writes=[r_st(rdc, 4)])
                P.add("dve", lambda e: e.tensor_tensor(
                    out=bo3[:, :, h * 64:(h + 1) * 64], in0=O3[:, :, 0:64],
                    in1=stc(rdc, 4).unsqueeze(2).broadcast_to([128, 4, 64]), op=ALU.mult),
                    reads=[r_ps(TB7), r_st(rdc, 4)],
                    writes=[r_bo(tb, h * 64, (h + 1) * 64) for tb in range(4)])
            return tail

        for h in range(8):
            ob = 5 + (h % 2)
            OT = ps[0:65, ob, :]
            pend = []

            def emit_pv(item, h=h, ob=ob, OT=OT):
                j, r, es_ = item
                c0 = max(0, r) * 128

                def pe_pv(e, j=j, c0=c0, es_=es_):
                    return e.matmul(out=OT[:, c0:512], lhsT=VA4[:, j, h, :], rhs=ET[:, es_ * 512 + c0:(es_ + 1) * 512],
                                    start=(j == 0), stop=(j == nkb - 1), skip_group_check=True)
                P.add("pe", pe_pv, reads=[R("ET", (es_ * 512 + c0) * 2, (es_ + 1) * 1024), r_VA(j, h)],
                      writes=[r_ps(ob)])

            for j in range(nkb):
                r = j - blk0
                c0 = max(0, r) * 128
                sbk = sbanks[scnt[0] % 3]
                scnt[0] += 1
                es_ = ecnt[0] % 4
                ecnt[0] += 1

                def pe_qk(e, h=h, j=j, r=r, c0=c0, sbk=sbk):
                    ins = e.matmul(out=ps[:, sbk, c0:512], lhsT=KT3[0:70, h, j * 128:(j + 1) * 128],
                                   rhs=QT3[0:70, h, c0:512], start=True, stop=(r < 0), skip_group_check=True)
                    if r >= 0:
                        ins = e.matmul(out=ps[:, sbk, c0:c0 + 128], lhsT=ident, rhs=maskneg, start=False, stop=True,
                                       skip_group_check=True)
                    return ins
                P.add("pe", pe_qk, reads=[r_KT(h, j), r_QT(h), R("cb", 0, 512)], writes=[r_ps(sbk)])
                P.add("act", lambda e, c0=c0, sbk=sbk, es_=es_: e.activation(
                    out=ET[:, es_ * 512 + c0:(es_ + 1) * 512], in_=ps[:, sbk, c0:512], func=AF.Exp),
                    reads=[r_ps(sbk)], writes=[R("ET", (es_ * 512 + c0) * 2, (es_ + 1) * 1024)])
                pend.append((j, r, es_))
                if len(pend) > 2:
                    emit_pv(pend.pop(0))
                if j == 1 and tails:
                    tails.pop(0)()
            while pend:
                emit_pv(pend.pop(0))
            P.add("dve", lambda e, h=h, OT=OT: e.tensor_copy(out=mt[0:65, (h % 2) * 512:(h % 2 + 1) * 512], in_=OT),
                  reads=[r_ps(ob)], writes=[R("mt", (h % 2) * 2048, (h % 2 + 1) * 2048)])
            tails.append(make_tail(h, ob))
            gmlp_piece(h)
        while tails:
            tails.pop(0)()

        for tb in range(4):
            if tb < 2:
                P.add("act", lambda e, tb=tb: e.activation(out=junk[:, 0:512], in_=bo3[:, tb, :], func=AF.Square,
                                                           accum_out=stc(ST_SSB + tb)),
                      reads=[r_bo(tb)], writes=[R("junk", 0, 1024), r_st(ST_SSB + tb)])
            else:
                P.add("dve", lambda e, tb=tb: e.scalar_tensor_tensor(
                    out=aout[:, :], in0=bo3[:, tb, :], scalar=1.0, in1=bo3[:, tb, :], op0=ALU.mult, op1=ALU.mult,
                    accum_out=stc(ST_SSB + tb)),
                    reads=[r_bo(tb)], writes=[R("aout", 0, 2048), r_st(ST_SSB + tb)])
            if debug:
                r0 = row0 + tb * 128
                P.add("pool", lambda e, r0=r0, tb=tb: e.dma_start(out=dbg2_d[r0:r0 + 128, 512:1024], in_=bo3[:, tb, :]),
                      reads=[r_bo(tb)], writes=[R("dbg2", r0, r0 + 128)], chan=ch_dbg2[1])
        rms_rstd(ST_SSB, 4, ST_RSB, 1.0 / 512, RMS_EPS)
        for tb in range(4):
            if tb % 2 == 0:
                P.add("act", lambda e, tb=tb: e.activation(out=merged3[:, tb, 512:1024], in_=bo3[:, tb, :], func=AF.Copy,
                                                           scale=stc(ST_RSB + tb)),
                      reads=[r_bo(tb), r_st(ST_RSB + tb)], writes=[R("merged", tb * 2048 + 1024, (tb + 1) * 2048)])
            else:
                P.add("dve", lambda e, tb=tb: e.tensor_scalar(out=merged3[:, tb, 512:1024], in0=bo3[:, tb, :],
                                                              scalar1=stc(ST_RSB + tb), scalar2=None, op0=ALU.mult),
                      reads=[r_bo(tb), r_st(ST_RSB + tb)], writes=[R("merged", tb * 2048 + 1024, (tb + 1) * 2048)])
        for tb in range(4):
            transpose_block(lambda k, tb=tb: merged3[:, tb, k * 128:(k + 1) * 128],
                            [R("merged", tb * 2048, (tb + 1) * 2048)], tb, tb % 2, G_GT_OUTN, r_actT_tb(tb))
        s0, sl0 = load_slot(s_wout3[:, :, 0:512], lambda a: a.rearrange("p (k c) -> p k c", k=8), [R("s_wout", 0, 512)])
        s1, sl1 = load_slot(s_wout3[:, :, 512:1024], lambda a: a.rearrange("p (k c) -> p k c", k=8), [R("s_wout", 512, 1024)])
        wsl = [sl0.rearrange("p (k c) -> p k c", k=8), sl1.rearrange("p (k c) -> p k c", k=8)]
        obanks = [(2, 3), (4, 5), (6, 7), (0, 1)]
        htbank = [2, 3, 4, 5]
        def emit_pe_o(tb):
            banks = obanks[tb]

            def pe_o(e, tb=tb, banks=banks):
                ins = None
                for ct in range(2):
                    for k in range(8):
                        ins = e.matmul(out=ps[:, banks[ct], :], lhsT=actT3[:, k, tb * 128:(tb + 1) * 128],
                                       rhs=wsl[ct][:, k, :], start=(k == 0), stop=(k == 7))
                return ins
            P.add("pe", pe_o, reads=r_actT_tb(tb) + [r_ring(s0), r_ring(s1)], writes=[r_ps(banks[0]), r_ps(banks[1])])

        def emit_evac_o(tb):
            banks = obanks[tb]
            b0 = banks[0]
            rb = [r_ps(banks[0]), r_ps(banks[1])]
            P.add("act", lambda e, tb=tb, b0=b0: e.activation(out=junk[:, :].rearrange("p (c d) -> p c d", c=2),
                                                              in_=ps[:, b0:b0 + 2, :], func=AF.Square,
                                                              accum_out=stc(ST_SSM + tb)),
                  reads=rb, writes=[R("junk", 0, 2048), r_st(ST_SSM + tb)])
            rms_rstd(ST_SSM + tb, 1, ST_RSM + tb, 1.0 / D, RMS_EPS)
            P.add("dve", lambda e, tb=tb, b0=b0: e.scalar_tensor_tensor(
                out=mt[:, :].rearrange("p (c d) -> p c d", c=2), in0=ps[:, b0:b0 + 2, :], scalar=stc(ST_RSM + tb),
                in1=gains[:, G_POSTMIX:G_POSTMIX + 1024].rearrange("p (c d) -> p c d", c=2), op0=ALU.mult, op1=ALU.mult),
                reads=rb + [r_st(ST_RSM + tb), R("gains", G_POSTMIX * 4, (G_POSTMIX + 1024) * 4)],
                writes=[R("mt", 0, 4096)])
            P.add("dve", lambda e, tb=tb, hp=hp: e.tensor_tensor(
                out=hg4[:, hp, tb, :], in0=hg4[:, hp, tb, :], in1=mt[:, :], op=ALU.add),
                reads=[R("mt", 0, 4096), r_hg(hp, tb)], writes=[r_hg(hp, tb)])
            if debug:
                r0 = row0 + tb * 128
                P.add("pool", lambda e, tb=tb, r0=r0, hp=hp: e.dma_start(out=dbg_d[r0:r0 + 128, :], in_=hg4[:, hp, tb, :]),
                      reads=[r_hg(hp, tb)], writes=[R("dbg", r0, r0 + 128)], chan=ch_dbg[tb])
        def emit_evac_o2(tb):
            P.add("act", lambda e, tb=tb, hp=hp: e.activation(out=junk[:, :], in_=hg4[:, hp, tb, :], func=AF.Square,
                                                              accum_out=stc(ST_SSH + tb)),
                  reads=[r_hg(hp, tb)], writes=[R("junk", 0, 2048), r_st(ST_SSH + tb)])
            rms_rstd(ST_SSH + tb, 1, ST_RSH + tb, 1.0 / D, RMS_EPS)
            par = tb % 2
            P.add("act", lambda e, tb=tb, par=par, hp=hp: e.activation(
                out=xn[:, par * D:(par + 1) * D], in_=hg4[:, hp, tb, :], func=AF.Copy, scale=stc(ST_RSH + tb)),
                reads=[r_hg(hp, tb), r_st(ST_RSH + tb)], writes=[R("xn", par * 2048, (par + 1) * 2048)])
            transpose_block(lambda k, par=par: xn[:, par * D + k * 128: par * D + (k + 1) * 128],
                            [R("xn", par * 2048, (par + 1) * 2048)], tb, htbank[tb], G_GT_FFN, r_actT_tb(tb))

        emit_pe_o(0)
        emit_pe_o(1)
        emit_pe_o(2)
        emit_pe_o(3)
        emit_evac_o(0)
        emit_evac_o(1)
        emit_evac_o2(0)
        emit_evac_o(2)
        emit_evac_o2(1)
        emit_evac_o(3)
        emit_evac_o2(2)
        emit_evac_o2(3)

        if g + 1 < ngroups:
            xload(g + 1)

        if g == 0:
            issue_convs(100)
        fcnt = [0]
        for sl_i in range(11):
            s, sl = load_slot(s_wfi4[:, :, :, sl_i * 256:(sl_i + 1) * 256],
                              lambda a: a.rearrange("p (k u c) -> p k u c", k=8, u=2),
                              [R("s_wfi", sl_i * 256, (sl_i + 1) * 256)], split_u=True)
            w4 = sl.rearrange("p (k u c) -> p k u c", k=8, u=2)
            for fi in range(2):
                f = sl_i * 2 + fi
                par = fcnt[0] % 2
                fcnt[0] += 1
                bg, bu = (2, 3) if par == 0 else (4, 5)

                def pe_f2(e, fi=fi, bg=bg, bu=bu, w4=w4):
                    ins = None
                    for k in range(8):
                        ins = e.matmul(out=ps[:, bg, :], lhsT=w4[:, k, 0, fi * 128:(fi + 1) * 128], rhs=actT3[:, k, :],
                                       start=(k == 0), stop=(k == 7))
                    for k in range(8):
                        ins = e.matmul(out=ps[:, bu, :], lhsT=w4[:, k, 1, fi * 128:(fi + 1) * 128], rhs=actT3[:, k, :],
                                       start=(k == 0), stop=(k == 7))
                    return ins
                P.add("pe", pe_f2, reads=r_actT_all() + [r_ring(s)], writes=[r_ps(bg), r_ps(bu)])
                sgp = sg[:, par * 512:(par + 1) * 512]
                rsg = R("mt", par * 2048, (par + 1) * 2048)
                P.add("act", lambda e, bg=bg, sgp=sgp: e.activation(out=sgp, in_=ps[:, bg, :], func=AF.Silu),
                      reads=[r_ps(bg)], writes=[rsg])
                P.add("dve", lambda e, f=f, bu=bu, sgp=sgp: e.tensor_tensor(out=ffT3[:, f, :], in0=sgp, in1=ps[:, bu, :],
                                                                            op=ALU.mult),
                      reads=[rsg, r_ps(bu)], writes=[r_ffT(f)])

    G_BANKS = {0: (2, 3), 1: (4, 5), 2: (6, 7), 3: (0, 1)}

    def stageG_pe(g, pss):
        tbs = (0, 1) if pss == 0 else (2, 3)
        for sl_i in range(6):
            f0, f1 = sl_i * 4, min(NFT, sl_i * 4 + 4)
            s, sl = load_slot(s_wfo3[:, f0:f1, :], lambda a, n=f1 - f0: a[:, 0:n * 1024].rearrange("p (k c) -> p k c", k=n),
                              [R("s_wfo", f0, f1)])
            w3 = sl[:, 0:(f1 - f0) * 1024].rearrange("p (k c) -> p k c", k=f1 - f0)

            def pe_fo(e, f0=f0, f1=f1, w3=w3, tbs=tbs):
                ins = None
                for tb in tbs:
                    for ct in range(2):
                        for f in range(f0, f1):
                            ins = e.matmul(out=ps[:, G_BANKS[tb][ct], :], lhsT=ffT3[:, f, tb * 128:(tb + 1) * 128],
                                           rhs=w3[:, f - f0, ct * 512:(ct + 1) * 512],
                                           start=(f == 0), stop=(f == NFT - 1))
                return ins
            P.add("pe", pe_fo, reads=[r_ffT(f) for f in range(f0, f1)] + [r_ring(s)],
                  writes=[r_ps(b) for tb in tbs for b in G_BANKS[tb]])

    def stageG_evac(g, pss):
        hp = g % 2
        tbs = (0, 1) if pss == 0 else (2, 3)
        for tb in tbs:
            b0 = G_BANKS[tb][0]
            rb = [r_ps(b0), r_ps(b0 + 1)]
            P.add("act", lambda e, tb=tb, b0=b0: e.activation(out=junk[:, :].rearrange("p (c d) -> p c d", c=2),
                                                              in_=ps[:, b0:b0 + 2, :], func=AF.Square,
                                                              accum_out=stc(ST_SSO + tb)),
                  reads=rb, writes=[R("junk", 0, 2048), r_st(ST_SSO + tb)])
            rms_rstd(ST_SSO + tb, 1, ST_RSO + tb, 1.0 / D, RMS_EPS)
            P.add("dve", lambda e, tb=tb, b0=b0: e.scalar_tensor_tensor(
                out=mt[:, :].rearrange("p (c d) -> p c d", c=2), in0=ps[:, b0:b0 + 2, :], scalar=stc(ST_RSO + tb),
                in1=gains[:, G_POSTFFN:G_POSTFFN + 1024].rearrange("p (c d) -> p c d", c=2), op0=ALU.mult, op1=ALU.mult),
                reads=rb + [r_st(ST_RSO + tb), R("gains", G_POSTFFN * 4, (G_POSTFFN + 1024) * 4)],
                writes=[R("mt", 0, 4096)])
            P.add("dve", lambda e, tb=tb, hp=hp: e.tensor_tensor(
                out=hg4[:, hp, tb, :], in0=hg4[:, hp, tb, :], in1=mt[:, :], op=ALU.add),
                reads=[R("mt", 0, 4096), r_hg(hp, tb)], writes=[r_hg(hp, tb)])
            r0 = g * 512 + tb * 128
            P.add("pool", lambda e, tb=tb, r0=r0, hp=hp: e.dma_start(out=out_d[r0:r0 + 128, :], in_=hg4[:, hp, tb, :]),
                  reads=[r_hg(hp, tb)], writes=[R("out", r0, r0 + 128)], chan=ch_o[hp * 4 + tb])

    xload(0)
    late_consts()
    stageA(0, (2, 3, 4, 5))
    conv_win_rest()
    late_setup()
    for g in range(ngroups):
        stagesBF(g)
        if upto != "G":
            continue
        stageG_pe(g, 0)
        stageG_pe(g, 1)
        stageG_evac(g, 0)
        if g + 1 < ngroups:
            stageA(g + 1, (2, 3, 4, 5))
        stageG_evac(g, 1)

    P.add("pool", None, reads=[R("out", 0, TOK)] + ([R("dbg", 0, TOK), R("dbg2", 0, TOK)] if debug else []))

    eng_sems = {e: sem("es_" + e) for e in Prog.ENGS}
    P.finalize(eng_sems)
    with nc.Block() as block:
        @block.tensor
        def _(e):
            P.run_engine(e, "pe")

        @block.scalar
        def _(e):
            P.run_engine(e, "act")

        @block.vector
        def _(e):
            P.run_engine(e, "dve")

        @block.gpsimd
        def _(e):
            P.run_engine(e, "pool")

        @block.sync
        def _(e):
            P.run_engine(e, "sp")
    es.close()
    return nc


def _host_consts(pre_mix_gain, ln_v_gain, ln_v_bias, w_spatial, b_spatial, b_forget, out_norm_a_gain,
                 out_norm_b_gain, post_mix_gain, pre_ffn_gain, post_ffn_gain):
    f = np.float32
    gains = np.zeros((128, GW), f)
    gains[:, G_POSTMIX:G_POSTMIX + 1024] = np.asarray(post_mix_gain, f).reshape(1, 1024)
    gains[:, G_POSTFFN:G_POSTFFN + 1024] = np.asarray(post_ffn_gain, f).reshape(1, 1024)
    gains[:, G_LNVG:G_LNVG + 512] = np.asarray(ln_v_gain, f).reshape(1, 512)
    gains[:, G_LNVB:G_LNVB + 512] = np.asarray(ln_v_bias, f).reshape(1, 512)
    bs = np.asarray(b_spatial, f).reshape(8, 128)
    gains[:, G_BS:G_BS + 512] = np.repeat(bs.T[:, :, None], 64, axis=2).reshape(128, 512)
    gains[:, G_BFOR:G_BFOR + 8] = np.asarray(b_forget, f).reshape(1, 8)
    gains[:, G_GT_PRE:G_GT_PRE + 8] = np.asarray(pre_mix_gain, f).reshape(8, 128).T
    outn = np.concatenate([np.asarray(out_norm_a_gain, f).reshape(-1), np.asarray(out_norm_b_gain, f).reshape(-1)])
    gains[:, G_GT_OUTN:G_GT_OUTN + 8] = outn.reshape(8, 128).T
    gains[:, G_GT_FFN:G_GT_FFN + 8] = np.asarray(pre_ffn_gain, f).reshape(8, 128).T
    gains[:, G_NEGH] = -0.5
    s = np.arange(128)[:, None]
    t = np.arange(128)[None, :]
    cf = np.zeros((128, 512), f)
    cf[:, 384:512] = np.eye(128, dtype=f)
    cf[:, 0:128] = (t >= s).astype(f)
    cf[:, 128:256] = -(s <= t).astype(f)
    cf[:, 256:384] = -1.0
    cb = np.zeros((128, 256), f)
    cb[:, 0:128] = np.eye(128, dtype=f)
    cb[:, 128:256] = np.where(s > t, -30000.0, 0.0)
    cb = cb.astype(ml_dtypes.bfloat16)
    ws = np.asarray(w_spatial, f).reshape(8, 128, 128)
    wsT = np.ascontiguousarray(ws.transpose(2, 0, 1)).reshape(128, 1024)
    return gains, cf, cb, wsT


_NC_CACHE = {}


def kernel(x, pre_mix_gain, w_in, ln_v_gain, ln_v_bias, w_spatial, b_spatial, b_forget, out_norm_a_gain,
           out_norm_b_gain, w_out, post_mix_gain, pre_ffn_gain, w_ffn_in, w_ffn_out, post_ffn_gain):
    x = np.asarray(x, np.float32)
    gains, cf, cb, wsT = _host_consts(pre_mix_gain, ln_v_gain, ln_v_bias, w_spatial, b_spatial, b_forget,
                                      out_norm_a_gain, out_norm_b_gain, post_mix_gain, pre_ffn_gain, post_ffn_gain)
    common = {
        "w_in": np.ascontiguousarray(np.asarray(w_in, np.float32).reshape(D, INC)),
        "w_out": np.ascontiguousarray(np.asarray(w_out, np.float32).reshape(D, D)),
        "w_ffn_in": np.ascontiguousarray(np.asarray(w_ffn_in, np.float32).reshape(D, 2 * DFF)),
        "w_ffn_out": np.ascontiguousarray(np.asarray(w_ffn_out, np.float32).reshape(DFF, D)),
        "gains": gains, "cf": cf, "cb": cb, "wsT": wsT,
    }
    xs = x.reshape(NCORES, TOK, D)
    in_maps = [dict(common, x=np.ascontiguousarray(xs[c])) for c in range(NCORES)]
    if "nc" not in _NC_CACHE:
        _NC_CACHE["nc"] = build_nc()
    res = run_bass_kernel_spmd(_NC_CACHE["nc"], in_maps, core_ids=list(range(NCORES)))
    out = np.stack([np.asarray(r["out"]) for r in res.results], axis=0)
    return out.reshape(16, SEQ, D).astype(np.float32)
```

```python
import numpy as np
import ml_dtypes
from contextlib import ExitStack
import concourse.bass as bass
import concourse.mybir as mybir
from concourse.bass_utils import run_bass_kernel_spmd

F32 = mybir.dt.float32
BF16 = mybir.dt.bfloat16
AF = mybir.ActivationFunctionType
ALU = mybir.AluOpType

D = 1024
SEQ = 2048
NCORES = 8
TOK = 4096
INC = 2568
DFF = 2816
NFT = 22
RMS_EPS = 1e-6
LN_EPS = 1e-5

G_POSTMIX = 0
G_POSTFFN = 1024
G_LNVG = 2048
G_LNVB = 2560
G_BS = 3072
G_BFOR = 3584
G_GT_PRE = 3592
G_GT_OUTN = 3600
G_GT_FFN = 3608
G_NEGH = 3616
GW = 3624

NSLOT = 4
import os as _os0
SAME_ENG_SYNC = int(_os0.environ.get("SAME_ENG_SYNC", "2"))
SLOT_ELEMS = 4096


class Chan:
    def __init__(self, sem):
        self.sem = sem
        self.count = 0


class Op:
    __slots__ = ("eng", "fn", "deps", "chan", "ninc", "signal", "sigval", "idx", "hard")


class Prog:
    ENGS = ("pe", "act", "dve", "pool", "sp")

    def __init__(self):
        self.ops = []
        self.recs = {}

    def add(self, eng, fn, reads=(), writes=(), chan=None, ninc=1):
        op = Op()
        op.eng = eng
        op.fn = fn
        op.chan = chan
        op.ninc = ninc
        op.signal = False
        op.sigval = None
        op.idx = len(self.ops)
        reads = list(reads)
        writes = list(writes)
        psr = [r for r in reads if r[0] == "ps"]
        if psr:
            reads = [r for r in reads if r[0] != "ps"]
            writes = writes + psr
        writes = [(m, (lo // 2048) * 2048, -((-hi) // 2048) * 2048) if m == "ps" else (m, lo, hi) for (m, lo, hi) in writes]
        deps = {}
        hard = {}
        for (mem, lo, hi) in reads:
            for rec in self.recs.get(mem, ()):
                if rec[3] and rec[0] < hi and lo < rec[1]:
                    deps[rec[2].idx] = rec[2]
                    hard[rec[2].idx] = True
        for (mem, lo, hi) in writes:
            for rec in self.recs.get(mem, ()):
                if rec[0] < hi and lo < rec[1]:
                    deps[rec[2].idx] = rec[2]
                    if rec[3]:
                        hard[rec[2].idx] = True
        op.deps = list(deps.values())
        op.hard = hard
        for (mem, lo, hi) in writes:
            lst = self.recs.setdefault(mem, [])
            lst[:] = [r for r in lst if not (lo <= r[0] and r[1] <= hi)]
            lst.append((lo, hi, op, True))
        for (mem, lo, hi) in reads:
            lst = self.recs.setdefault(mem, [])
            lst[:] = [r for r in lst if not ((not r[3]) and r[0] == lo and r[1] == hi
                                             and r[2].eng == eng and r[2].chan is None and chan is None)]
            lst.append((lo, hi, op, False))
        self.ops.append(op)
        return op

    @staticmethod
    def needs_sync(op, d):
        if d.eng != op.eng or op.chan is not None:
            return True
        if op.eng == "pe":
            return False
        if SAME_ENG_SYNC == 2:
            return True
        return bool(SAME_ENG_SYNC) and op.hard.get(d.idx, False)

    def finalize(self, eng_sems):
        for op in self.ops:
            for d in op.deps:
                if d.chan is not None:
                    continue
                if self.needs_sync(op, d):
                    d.signal = True
        cnt = {e: 0 for e in self.ENGS}
        for op in self.ops:
            if op.chan is not None:
                op.chan.count += 16 * op.ninc
                op.sigval = op.chan.count
            elif op.signal:
                cnt[op.eng] += 1
                op.sigval = cnt[op.eng]
        self.eng_sems = eng_sems

    def run_engine(self, e, eng):
        seen = {}
        for op in self.ops:
            if op.eng != eng:
                continue
            need = {}
            for d in op.deps:
                if d.chan is not None:
                    key = id(d.chan)
                    sem = d.chan.sem
                elif self.needs_sync(op, d):
                    key = d.eng
                    sem = self.eng_sems[d.eng]
                else:
                    continue
                if key not in need or need[key][1] < d.sigval:
                    need[key] = (sem, d.sigval)
            for key, (sem, val) in need.items():
                if seen.get(key, 0) >= val:
                    continue
                seen[key] = val
                e.wait_ge(sem, val)
            if op.fn is None:
                continue
            ins = op.fn(e)
            if op.chan is not None:
                if not isinstance(ins, (list, tuple)):
                    ins = [ins]
                assert len(ins) == op.ninc
                for i in ins:
                    i.then_inc(op.chan.sem, 16)
            elif op.signal:
                ins.then_inc(self.eng_sems[eng], 1)


def build_nc(ngroups=8, debug=False, upto="G"):
    nc = bass.Bass("TRN2", target_bir_lowering=False)
    P = Prog()
    es = ExitStack()

    def dram_in(name, shape, dt):
        return nc.dram_tensor(name, shape, dt, kind="ExternalInput").ap()

    x_d = dram_in("x", [TOK, D], F32)
    w_in_d = dram_in("w_in", [D, INC], F32)
    w_out_d = dram_in("w_out", [D, D], F32)
    w_fi_d = dram_in("w_ffn_in", [D, 2 * DFF], F32)
    w_fo_d = dram_in("w_ffn_out", [DFF, D], F32)
    gains_d = dram_in("gains", [128, GW], F32)
    cf_d = dram_in("cf", [128, 512], F32)
    cb_d = dram_in("cb", [128, 256], BF16)
    wsT_d = dram_in("wsT", [128, 1024], F32)
    out_d = nc.dram_tensor("out", [TOK, D], F32, kind="ExternalOutput").ap()
    dbg_d = None
    if debug:
        dbg_d = nc.dram_tensor("dbg", [TOK, D], F32, kind="ExternalOutput").ap()
        dbg2_d = nc.dram_tensor("dbg2", [TOK, D], F32, kind="ExternalOutput").ap()

    s_win = nc.dram_tensor("s_win", [128, 8 * INC], BF16).ap()
    s_wout = nc.dram_tensor("s_wout", [128, 8 * D], BF16).ap()
    s_wfi = nc.dram_tensor("s_wfi", [128, 8 * 2 * DFF], BF16).ap()
    s_wfo = nc.dram_tensor("s_wfo", [128, NFT * D], BF16).ap()
    s_win3 = s_win.rearrange("p (k c) -> p k c", k=8)
    s_wout3 = s_wout.rearrange("p (k c) -> p k c", k=8)
    s_wfi4 = s_wfi.rearrange("p (k u c) -> p k u c", k=8, u=2)
    s_wfo3 = s_wfo.rearrange("p (k c) -> p k c", k=NFT)
    w_in3 = w_in_d.rearrange("(k p) c -> p k c", p=128)
    w_out3 = w_out_d.rearrange("(k p) c -> p k c", p=128)
    w_fi4 = w_fi_d.rearrange("(k p) (u c) -> p k u c", p=128, u=2)
    w_fo3 = w_fo_d.rearrange("(k p) c -> p k c", p=128)

    def sb(name, shape, dt):
        return es.enter_context(nc.sbuf_tensor("sb_" + name, shape, dt))

    def sem(name):
        return es.enter_context(nc.semaphore(name))

    def chan(name):
        return Chan(sem(name))

    gains = sb("gains", [128, GW], F32)
    cf = sb("cf", [128, 512], F32)
    cb = sb("cb", [128, 256], BF16)
    wsT_b = sb("wsT_b", [128, 1024], BF16)
    win_f = sb("win_f", [128, 64], BF16)
    KT = sb("KT", [128, 8 * SEQ], BF16)
    VA = sb("VA", [128, 16 * 8 * 65], BF16)
    hg = sb("hg", [128, 2 * 4 * D], F32)
    actT = sb("actT", [128, 8 * 512], BF16)
    xn = sb("xn", [128, 2 * D], BF16)
    junk = sb("junk", [128, D], BF16)
    ug = sb("ug", [128, 4 * 512], F32)
    qaug = sb("qaug", [128, 4 * 8 * 70], BF16)
    kaug = sb("kaug", [128, 4 * 8 * 70], BF16)
    arena = sb("arena", [128, NFT * 512], BF16)
    wsT_f = arena[:, 0:2048].bitcast(F32)
    ET = sb("ET", [128, 4 * 512], BF16)
    merged = sb("merged", [128, 4 * D], BF16)
    aout = sb("aout", [128, 512], F32)
    mt = sb("mt", [128, 2 * 512], F32)
    vt = mt
    sg = mt
    ring = sb("ring", [128, NSLOT * SLOT_ELEMS], BF16)
    st = sb("st", [128, 320], F32)
    Rsum = sb("Rsum", [128, 8], F32)
    ps = es.enter_context(nc.psum_tensor("ps", [128, 8, 512], F32))

    ident = cb[:, 0:128]
    maskneg = cb[:, 128:256]
    mask01 = cf[:, 0:128]
    trineg = cf[:, 128:256]
    onesneg = cf[:, 256:384]
    identf = cf[:, 384:512]

    KT3 = KT[:, :].rearrange("p (h t) -> p h t", h=8)
    VA4 = VA[:, :].rearrange("p (b h c) -> p b h c", b=16, h=8)
    hg4 = hg[:, :].rearrange("p (a b d) -> p a b d", a=2, b=4)
    actT3 = actT[:, :].rearrange("p (k t) -> p k t", k=8)
    ug3 = ug[:, :].rearrange("p (b d) -> p b d", b=4)
    qaug4 = qaug[:, :].rearrange("p (b h c) -> p b h c", b=4, h=8)
    kaug4 = kaug[:, :].rearrange("p (b h c) -> p b h c", b=4, h=8)
    QT3 = arena[:, 0:4096].rearrange("p (h t) -> p h t", h=8)
    bo3 = arena[:, 4096:8192].bitcast(F32).rearrange("p (b d) -> p b d", b=4)
    vln3 = arena[:, 8192:10240].rearrange("p (b d) -> p b d", b=4)
    ffT3 = arena[:, :].rearrange("p (f t) -> p f t", f=NFT)
    merged3 = merged[:, :].rearrange("p (b d) -> p b d", b=4)
    vt4 = merged[:, :].bitcast(F32).rearrange("p (b d) -> p b d", b=4)

    def R(mem, lo, hi):
        return (mem, lo, hi)

    def r_hg(hp, tb, lo=0, hi=D):
        return R("hg", ((hp * 4 + tb) * D + lo) * 4, ((hp * 4 + tb) * D + hi) * 4)

    def r_actT_tb(tb):
        return [R("actT", (k * 512 + tb * 128) * 2, (k * 512 + tb * 128 + 128) * 2) for k in range(8)]

    def r_actT_all():
        return [R("actT", 0, 8192)]

    def r_ps(bank, lo=0, hi=2048):
        return R("ps", bank * 2048 + lo, bank * 2048 + hi)

    def r_QT(h=None):
        if h is None:
            return R("arena", 0, 8192)
        return R("arena", h * 1024, (h + 1) * 1024)

    def r_bo(tb, lo=0, hi=512):
        return R("arena", 8192 + (tb * 512 + lo) * 4, 8192 + (tb * 512 + hi) * 4)

    def r_vln(tb):
        return R("arena", 16384 + tb * 1024, 16384 + (tb + 1) * 1024)

    def r_ffT(f):
        return R("arena", f * 1024, (f + 1) * 1024)

    def r_ring(s):
        return R("ring", s * SLOT_ELEMS * 2, (s + 1) * SLOT_ELEMS * 2)

    def r_st(c, n=1):
        return R("st", c * 4, (c + n) * 4)

    def r_KT(h, j):
        return R("KT", (h * SEQ + j * 128) * 2, (h * SEQ + (j + 1) * 128) * 2)

    def r_VA(j, h=None):
        if h is None:
            return R("VA", j * 8 * 65 * 2, (j + 1) * 8 * 65 * 2)
        return R("VA", (j * 8 + h) * 65 * 2, (j * 8 + h + 1) * 65 * 2)

    ST_SSX, ST_RSX = 0, 4
    ST_BN = 8
    ST_MV = 32
    ST_RSV = 40
    ST_SSA, ST_RSA = 44, 48
    ST_SSB, ST_RSB = 52, 56
    ST_SSM, ST_RSM = 60, 68
    ST_SSH, ST_RSH = 72, 76
    ST_SSO, ST_RSO = 80, 88
    ST_Z = 96
    ST_A = 128
    ST_SP = 160
    ST_C = 192
    ST_R1 = 256
    ST_RDEN = 232
    ST_TMP = 240

    def stc(c, n=1):
        return st[:, c:c + n]

    negh = gains[:, G_NEGH:G_NEGH + 1]

    ch_const = [chan("c_g"), chan("c_cf"), chan("c_cb"), chan("c_ws"), chan("c_wf")]
    ch_ring = [chan(f"c_ring{i}") for i in range(NSLOT)]
    ch_x = [chan(f"c_x{i}") for i in range(8)]
    ch_o = [chan(f"c_o{i}") for i in range(8)]
    ch_dbg = [chan(f"c_d{i}") for i in range(4)] if debug else None
    ch_dbg2 = [chan(f"c_e{i}") for i in range(2)] if debug else None

    P.add("sp", lambda e: e.dma_start(out=cb[:, :], in_=cb_d), writes=[R("cb", 0, 512)], chan=ch_const[2])
    P.add("sp", lambda e: e.dma_start(out=gains[:, G_BFOR:GW], in_=gains_d[:, G_BFOR:GW]),
          writes=[R("gains", G_BFOR * 4, GW * 4)], chan=ch_const[0])
    ch_gbig = chan("c_gbig")

    def late_consts():
        P.add("sp", lambda e: e.dma_start(out=gains[:, 0:G_BFOR], in_=gains_d[:, 0:G_BFOR]),
              writes=[R("gains", 0, G_BFOR * 4)], chan=ch_gbig)
        P.add("sp", lambda e: e.dma_start(out=cf[:, :], in_=cf_d), writes=[R("cf", 0, 2048)], chan=ch_const[1])
        P.add("sp", lambda e: e.dma_start(out=wsT_f[:, :], in_=wsT_d), writes=[R("arena", 0, 4096)], chan=ch_const[3])

    conv = []

    def add_conv(name, mem, lo, hi, out_ap, in_ap, split_u=False):
        c = chan("cv_" + name)
        if split_u:
            P.add("pool", lambda e: [e.dma_start(out=out_ap[:, :, u, :], in_=in_ap[:, :, u, :]) for u in range(2)],
                  writes=[R(mem, lo, hi)], chan=c, ninc=2)
        else:
            P.add("pool", lambda e: e.dma_start(out=out_ap, in_=in_ap), writes=[R(mem, lo, hi)], chan=c)

    def conv_win(c):
        add_conv(f"win{c}", "s_win", c * 512, (c + 1) * 512,
                 s_win3[:, :, c * 512:(c + 1) * 512], w_in3[:, :, c * 512:(c + 1) * 512])

    def conv_win_rest():
        for c in range(1, 5):
            conv_win(c)
        P.add("pool", lambda e: e.dma_start(out=win_f[:, :].rearrange("p (k c) -> p k c", k=8),
                                            in_=w_in3[:, :, 2560:2568]),
              writes=[R("win_f", 0, 128)], chan=ch_const[4])
    conv_win(0)
    conv_q = []
    for c in range(2):
        conv_q.append((f"wout{c}", "s_wout", c * 512, (c + 1) * 512,
                       s_wout3[:, :, c * 512:(c + 1) * 512], w_out3[:, :, c * 512:(c + 1) * 512], False))
    for s in range(11):
        conv_q.append((f"wfi{s}", "s_wfi", s * 256, (s + 1) * 256,
                       s_wfi4[:, :, :, s * 256:(s + 1) * 256], w_fi4[:, :, :, s * 256:(s + 1) * 256], True))
    for s in range(6):
        f0, f1 = s * 4, min(NFT, s * 4 + 4)
        conv_q.append((f"wfo{s}", "s_wfo", f0, f1, s_wfo3[:, f0:f1, :], w_fo3[:, f0:f1, :], False))

    def issue_convs(n):
        for _ in range(n):
            if conv_q:
                nm, mem, lo, hi, oap, iap, su = conv_q.pop(0)
                add_conv(nm, mem, lo, hi, oap, iap, split_u=su)

    def late_setup():
        P.add("dve", lambda e: e.tensor_tensor(
            out=wsT_b[:, :].rearrange("p (g t) -> p g t", g=8),
            in0=wsT_f[:, :].rearrange("p (g t) -> p g t", g=8),
            in1=mask01.unsqueeze(1).broadcast_to([128, 8, 128]), op=ALU.mult),
            reads=[R("arena", 0, 4096), R("cf", 0, 512)], writes=[R("wsT_b", 0, 2048)])
        P.add("pool", lambda e: e.memset(VA4[:, :, :, 64:65], 1.0), writes=[R("VA", 0, 16 * 8 * 65 * 2)])


    ring_state = {"next": 0}

    def load_slot(src_ap, view_fn, src_regions, split_u=False):
        s = ring_state["next"] % NSLOT
        ring_state["next"] += 1
        base = ring[:, s * SLOT_ELEMS:(s + 1) * SLOT_ELEMS]
        dst = view_fn(base)
        if split_u:
            P.add("sp", lambda e: [e.dma_start(out=dst[:, :, u, :], in_=src_ap[:, :, u, :]) for u in range(2)],
                  reads=src_regions, writes=[r_ring(s)], chan=ch_ring[s], ninc=2)
        else:
            P.add("sp", lambda e: e.dma_start(out=dst, in_=src_ap), reads=src_regions,
                  writes=[r_ring(s)], chan=ch_ring[s])
        return s, base

    def rms_rstd(ss_col, n, rs_col, inv_n, eps):
        P.add("pool", lambda e: e.tensor_scalar(out=stc(ST_TMP, n), in0=stc(ss_col, n), scalar1=inv_n, scalar2=eps,
                                                op0=ALU.mult, op1=ALU.add),
              reads=[r_st(ss_col, n)], writes=[r_st(ST_TMP, n)])
        P.add("pool", lambda e: e.tensor_tensor(out=stc(rs_col, n), in0=stc(ST_TMP, n),
                                                in1=negh.broadcast_to([128, n]) if n > 1 else negh, op=ALU.pow),
              reads=[r_st(ST_TMP, n), R("gains", G_NEGH * 4, G_NEGH * 4 + 4)], writes=[r_st(rs_col, n)])

    def transpose_block(src_ap_fn, src_regions, tb, tbank, gcol, dst_regions, evac_hook=None):
        psb = ps[:, tbank, :].bitcast(BF16).rearrange("p (k t) -> p k t", k=8)

        def pe_fn(e):
            ins = None
            for k in range(8):
                ins = e.transpose(out=psb[:, k, :], in_=src_ap_fn(k), identity=ident)
            return ins
        P.add("pe", pe_fn, reads=src_regions + [R("cb", 0, 256)], writes=[r_ps(tbank)])
        if evac_hook is not None:
            evac_hook()
        gT = gains[:, gcol:gcol + 8]
        P.add("dve", lambda e: e.tensor_tensor(
            out=actT3[:, :, tb * 128:(tb + 1) * 128], in0=psb,
            in1=gT.unsqueeze(2).broadcast_to([128, 8, 128]), op=ALU.mult),
            reads=[r_ps(tbank), R("gains", gcol * 4, gcol * 4 + 32)], writes=dst_regions)

    def xload(g):
        hp = g % 2
        for tb in range(4):
            r0 = g * 512 + tb * 128
            P.add("sp", lambda e, tb=tb, r0=r0, hp=hp: e.dma_start(out=hg4[:, hp, tb, :], in_=x_d[r0:r0 + 128, :]),
                  reads=[R("x", r0, r0 + 128)], writes=[r_hg(hp, tb)], chan=ch_x[hp * 4 + tb])

    def stageA(g, tbanks):
        hp = g % 2
        for tb in range(4):
            P.add("act", lambda e, tb=tb, hp=hp: e.activation(out=junk[:, :], in_=hg4[:, hp, tb, :], func=AF.Square,
                                                              accum_out=stc(ST_SSX + tb)),
                  reads=[r_hg(hp, tb)], writes=[R("junk", 0, 2048), r_st(ST_SSX + tb)])
        rms_rstd(ST_SSX, 4, ST_RSX, 1.0 / D, RMS_EPS)
        def emit_xn(tb):
            par = tb % 2
            P.add("dve", lambda e, tb=tb, par=par, hp=hp: e.tensor_scalar(
                out=xn[:, par * D:(par + 1) * D], in0=hg4[:, hp, tb, :], scalar1=stc(ST_RSX + tb), scalar2=None,
                op0=ALU.mult),
                reads=[r_hg(hp, tb), r_st(ST_RSX + tb)], writes=[R("xn", par * 2048, (par + 1) * 2048)])

        def emit_tr(tb):
            par = tb % 2
            transpose_block(lambda k, par=par: xn[:, par * D + k * 128: par * D + (k + 1) * 128],
                            [R("xn", par * 2048, (par + 1) * 2048)], tb, tbanks[tb], G_GT_PRE, r_actT_tb(tb),
                            evac_hook=hooks.get(tb))
        hooks = {0: lambda: emit_xn(2), 1: lambda: emit_xn(3)}
        emit_xn(0)
        emit_xn(1)
        emit_tr(0)
        emit_tr(1)
        emit_tr(2)
        emit_tr(3)

    def stagesBF(g):
        hp = g % 2
        seq = g // 4
        gi = g % 4
        row0 = g * 512
        blk0 = gi * 4

        if gi == 0:
            P.add("pool", lambda e: e.memset(Rsum[:, :], 0.0), writes=[R("Rsum", 0, 32)])

        P.add("pool", lambda e: e.memset(qaug4[:, :, :, 67:70], 1.0), writes=[R("qaug", 0, 4 * 8 * 70 * 2)])
        P.add("pool", lambda e: e.memset(kaug4[:, :, :, 64:67], 1.0), writes=[R("kaug", 0, 4 * 8 * 70 * 2)])
        pbank = [2, 3, 4]
        pcnt = [0]

        def inproj_tile(slot_ap, tb):
            b = pbank[pcnt[0] % 3]
            pcnt[0] += 1
            w3 = slot_ap.rearrange("p (k c) -> p k c", k=8)

            def pe_fn(e):
                ins = None
                for k in range(8):
                    ins = e.matmul(out=ps[:, b, :], lhsT=actT3[:, k, tb * 128:(tb + 1) * 128], rhs=w3[:, k, :],
                                   start=(k == 0), stop=(k == 7))
                return ins
            return b, pe_fn

        s, sl = load_slot(s_win3[:, :, 0:512], lambda a: a.rearrange("p (k c) -> p k c", k=8), [R("s_win", 0, 512)])
        for tb in range(4):
            b, pe_fn = inproj_tile(sl, tb)
            P.add("pe", pe_fn, reads=r_actT_tb(tb) + [r_ring(s)], writes=[r_ps(b)])
            P.add("act", lambda e, tb=tb, b=b: e.activation(out=ug3[:, tb, :], in_=ps[:, b, :], func=AF.Gelu),
                  reads=[r_ps(b)], writes=[R("ug", tb * 2048, (tb + 1) * 2048)])
        s, sl = load_slot(s_win3[:, :, 512:1024], lambda a: a.rearrange("p (k c) -> p k c", k=8), [R("s_win", 512, 1024)])
        vln_late = []
        for tb in range(4):
            b, pe_fn = inproj_tile(sl, tb)
            vtp = vt4[:, tb, :]
            rv = R("merged", tb * 2048, (tb + 1) * 2048)
            P.add("pe", pe_fn, reads=r_actT_tb(tb) + [r_ring(s)], writes=[r_ps(b)])
            P.add("act", lambda e, b=b, vtp=vtp: e.activation(out=vtp, in_=ps[:, b, :], func=AF.Gelu),
                  reads=[r_ps(b)], writes=[rv])
            P.add("dve", lambda e, tb=tb, vtp=vtp: e.bn_stats(out=stc(ST_BN + 6 * tb, 6), in_=vtp),
                  reads=[rv], writes=[r_st(ST_BN + 6 * tb, 6)])
            P.add("dve", lambda e, tb=tb: e.bn_aggr(out=stc(ST_MV + 2 * tb, 2), in_=stc(ST_BN + 6 * tb, 6)),
                  reads=[r_st(ST_BN + 6 * tb, 6)], writes=[r_st(ST_MV + 2 * tb, 2)])
            rms_rstd(ST_MV + 2 * tb + 1, 1, ST_RSV + tb, 1.0, LN_EPS)
            P.add("dve", lambda e, tb=tb, vtp=vtp: e.scalar_tensor_tensor(
                out=vtp, in0=vtp, scalar=stc(ST_MV + 2 * tb), in1=gains[:, G_LNVG:G_LNVG + 512],
                op0=ALU.subtract, op1=ALU.mult),
                reads=[rv, r_st(ST_MV + 2 * tb), R("gains", G_LNVG * 4, (G_LNVG + 512) * 4)], writes=[rv])
        if g == 0:
            issue_convs(6)
        FB = 5

        def pe_f(e):
            ins = None
            w3 = win_f[:, :].rearrange("p (k c) -> p k c", k=8)
            for tb in range(4):
                for k in range(8):
                    ins = e.matmul(out=ps[:, FB, tb * 8:(tb + 1) * 8], lhsT=actT3[:, k, tb * 128:(tb + 1) * 128],
                                   rhs=w3[:, k, :], start=(k == 0), stop=(k == 7), skip_group_check=True)
            return ins
        P.add("pe", pe_f, reads=r_actT_all() + [R("win_f", 0, 128)], writes=[r_ps(FB)])
        zz, aa, spp, ccc = stc(ST_Z, 32), stc(ST_A, 32), stc(ST_SP, 32), stc(ST_C, 32)
        P.add("dve", lambda e: e.scalar_tensor_tensor(
            out=zz.rearrange("p (t h) -> p t h", t=4), in0=ps[:, FB, 0:32].rearrange("p (t h) -> p t h", t=4), scalar=-1.0,
            in1=gains[:, G_BFOR:G_BFOR + 8].unsqueeze(1).broadcast_to([128, 4, 8]), op0=ALU.mult, op1=ALU.subtract),
            reads=[r_ps(FB), R("gains", G_BFOR * 4, G_BFOR * 4 + 32)], writes=[r_st(ST_Z, 32)])
        P.add("act", lambda e: e.activation(out=aa, in_=zz, func=AF.Abs), reads=[r_st(ST_Z, 32)], writes=[r_st(ST_A, 32)])
        P.add("act", lambda e: e.activation(out=aa, in_=aa, func=AF.Exp, scale=-1.0),
              reads=[r_st(ST_A, 32)], writes=[r_st(ST_A, 32)])
        P.add("act", lambda e: e.activation(out=aa, in_=aa, func=AF.Ln, bias=1.0),
              reads=[r_st(ST_A, 32)], writes=[r_st(ST_A, 32)])
        s, sl = load_slot(s_win3[:, :, 1024:1536], lambda a: a.rearrange("p (k c) -> p k c", k=8), [R("s_win", 1024, 1536)])
        for tb in range(4):
            b, pe_fn = inproj_tile(sl, tb)
            P.add("pe", pe_fn, reads=r_actT_tb(tb) + [r_ring(s)], writes=[r_ps(b)])
            P.add("dve", lambda e, tb=tb, b=b: e.tensor_scalar(
                out=qaug4[:, tb, :, 0:64], in0=ps[:, b, :].rearrange("p (h c) -> p h c", h=8),
                scalar1=0.125, scalar2=None, op0=ALU.mult),
                reads=[r_ps(b)], writes=[R("qaug", tb * 1120, (tb + 1) * 1120)])
        for tb in range(4):
            P.add("dve", lambda e, tb=tb: e.scalar_tensor_tensor(
                out=vln3[:, tb, :], in0=vt4[:, tb, :], scalar=stc(ST_RSV + tb), in1=gains[:, G_LNVB:G_LNVB + 512],
                op0=ALU.mult, op1=ALU.add),
                reads=[R("merged", tb * 2048, (tb + 1) * 2048), r_st(ST_RSV + tb), R("gains", G_LNVB * 4, (G_LNVB + 512) * 4)],
                writes=[r_vln(tb)])

        P.add("dve", lambda e: e.scalar_tensor_tensor(out=spp, in0=zz, scalar=0.0, in1=aa, op0=ALU.max, op1=ALU.add),
              reads=[r_st(ST_Z, 32), r_st(ST_A, 32)], writes=[r_st(ST_SP, 32)])
        CB = 7

        def pe_c(e):
            ins = None
            for tb in range(4):
                o = ps[:, CB, tb * 8:(tb + 1) * 8]
                e.matmul(out=o, lhsT=trineg, rhs=stc(ST_SP + 8 * tb, 8), start=True, stop=False, skip_group_check=True)
                for u in range(tb):
                    e.matmul(out=o, lhsT=onesneg, rhs=stc(ST_SP + 8 * u, 8), start=False, stop=False,
                             skip_group_check=True)
                ins = e.matmul(out=o, lhsT=onesneg, rhs=Rsum[:, :], start=False, stop=True, skip_group_check=True)
            return ins
        s, sl = load_slot(s_win3[:, :, 1536:2048], lambda a: a.rearrange("p (k c) -> p k c", k=8), [R("s_win", 1536, 2048)])
        for tb in range(4):
            b, pe_fn = inproj_tile(sl, tb)
            P.add("pe", pe_fn, reads=r_actT_tb(tb) + [r_ring(s)], writes=[r_ps(b)])
            P.add("act", lambda e, tb=tb, b=b: e.activation(
                out=kaug4[:, tb, :, 0:64], in_=ps[:, b, :].rearrange("p (h c) -> p h c", h=8), func=AF.Copy),
                reads=[r_ps(b)], writes=[R("kaug", tb * 1120, (tb + 1) * 1120)])
            if tb == 1:
                P.add("pe", pe_c, reads=[r_st(ST_SP, 32), R("Rsum", 0, 32), R("cf", 512, 1536)], writes=[r_ps(CB)])
        P.add("dve", lambda e: e.tensor_copy(out=ccc, in_=ps[:, CB, 0:32]), reads=[r_ps(CB)], writes=[r_st(ST_C, 32)])
        rq = R("qaug", 0, 4480)
        rk = R("kaug", 0, 4480)
        c3 = ccc.rearrange("p (t h) -> p t h", t=4)
        r13 = stc(ST_R1, 32).rearrange("p (t h) -> p t h", t=4)
        P.add("dve", lambda e: e.tensor_copy(out=qaug4[:, :, :, 64], in_=c3), reads=[r_st(ST_C, 32)], writes=[rq])
        P.add("dve", lambda e: e.tensor_tensor(out=r13, in0=c3, in1=qaug4[:, :, :, 64], op=ALU.subtract),
              reads=[r_st(ST_C, 32), rq], writes=[r_st(ST_R1, 32)])
        P.add("dve", lambda e: e.tensor_copy(out=qaug4[:, :, :, 65], in_=r13), reads=[r_st(ST_R1, 32)], writes=[rq])
        P.add("dve", lambda e: e.tensor_tensor(out=r13, in0=r13, in1=qaug4[:, :, :, 65], op=ALU.subtract),
              reads=[r_st(ST_R1, 32), rq], writes=[r_st(ST_R1, 32)])
        P.add("dve", lambda e: e.tensor_copy(out=qaug4[:, :, :, 66], in_=r13), reads=[r_st(ST_R1, 32)], writes=[rq])
        for tb in range(4):
            P.add("dve", lambda e, tb=tb: e.tensor_scalar(out=kaug4[:, tb, :, 67:70], in0=qaug4[:, tb, :, 64:67],
                                                          scalar1=-1.0, scalar2=None, op0=ALU.mult),
                  reads=[rq], writes=[rk])
        s, sl = load_slot(s_win3[:, :, 2048:2560], lambda a: a.rearrange("p (k c) -> p k c", k=8), [R("s_win", 2048, 2560)])
        for tb in range(4):
            b, pe_fn = inproj_tile(sl, tb)
            j = blk0 + tb
            P.add("pe", pe_fn, reads=r_actT_tb(tb) + [r_ring(s)], writes=[r_ps(b)])
            P.add("dve", lambda e, j=j, b=b: e.tensor_copy(
                out=VA4[:, j, :, 0:64], in_=ps[:, b, :].rearrange("p (h c) -> p h c", h=8)),
                reads=[r_ps(b)], writes=[r_VA(j)])
        gbanks = [2, 3, 4, 6]
        def emit_gmlp_pe():
            gbanks = [2, 3, 4, 6]
            for tb in range(4):
                gb = gbanks[tb]

                def pe_g(e, tb=tb, gb=gb):
                    ins = None
                    w3 = wsT_b[:, :].rearrange("p (g t) -> p g t", g=8)
                    for gg in range(8):
                        ins = e.matmul(out=ps[:, gb, gg * 64:(gg + 1) * 64], lhsT=w3[:, gg, :],
                                       rhs=vln3[:, tb, gg * 64:(gg + 1) * 64], start=True, stop=True, skip_group_check=True)
                    return ins
                P.add("pe", pe_g, reads=[R("wsT_b", 0, 2048), r_vln(tb)], writes=[r_ps(gb)])

        tbanks4 = [0, 1, 5, 7]
        tcnt = [0]
        for hp2 in range(4):
            for which in range(2):
                src4 = qaug4 if which == 0 else kaug4
                srcname = "qaug" if which == 0 else "kaug"
                if tcnt[0] == 4:
                    emit_gmlp_pe()
                    for tb_ in range(4):
                        P.add("dve", lambda e, tb=tb_, gb=gbanks[tb_]: e.tensor_tensor(
                            out=bo3[:, tb, :], in0=ps[:, gb, :], in1=gains[:, G_BS:G_BS + 512], op=ALU.add),
                            reads=[r_ps(gbanks[tb_]), R("gains", G_BS * 4, (G_BS + 512) * 4)], writes=[r_bo(tb_)])
                tbank = tbanks4[tcnt[0] % 4]
                tcnt[0] += 1
                pst = ps[0:70, tbank, :].bitcast(BF16).rearrange("p (h t) -> p h t", h=2)

                def pe_t(e, hp2=hp2, src4=src4, pst=pst):
                    ins = None
                    for hh in range(2):
                        for tb in range(4):
                            ins = e.transpose(out=pst[:, hh, tb * 128:(tb + 1) * 128], in_=src4[:, tb, 2 * hp2 + hh, :],
                                              identity=ident)
                    return ins
                P.add("pe", pe_t, reads=[R(srcname, 0, 4480), R("cb", 0, 256)], writes=[r_ps(tbank)])
                if which == 0:
                    P.add("act", lambda e, hp2=hp2, pst=pst: e.activation(out=QT3[0:70, 2 * hp2:2 * hp2 + 2, :], in_=pst,
                                                                          func=AF.Copy),
                          reads=[r_ps(tbank)], writes=[r_QT(2 * hp2), r_QT(2 * hp2 + 1)])
                else:
                    P.add("dve", lambda e, hp2=hp2, pst=pst: e.tensor_copy(
                        out=KT3[0:70, 2 * hp2:2 * hp2 + 2, blk0 * 128:(blk0 + 4) * 128], in_=pst),
                        reads=[r_ps(tbank)],
                        writes=[R("KT", (hh_ * SEQ + blk0 * 128) * 2, (hh_ * SEQ + (blk0 + 4) * 128) * 2)
                                for hh_ in (2 * hp2, 2 * hp2 + 1)])

        if g == 0:
            issue_convs(7)
        for tb in range(4):
            P.add("dve", lambda e, tb=tb: e.tensor_tensor(out=Rsum[:, :], in0=Rsum[:, :], in1=stc(ST_SP + 8 * tb, 8),
                                                          op=ALU.add),
                  reads=[R("Rsum", 0, 32), r_st(ST_SP + 8 * tb, 8)], writes=[R("Rsum", 0, 32)])

        for tb in range(4):
            P.add("dve", lambda e, tb=tb: e.tensor_tensor(out=ug3[:, tb, :], in0=bo3[:, tb, :], in1=ug3[:, tb, :],
                                                          op=ALU.mult),
                  reads=[r_bo(tb), R("ug", tb * 2048, (tb + 1) * 2048)], writes=[R("ug", tb * 2048, (tb + 1) * 2048)])
            if debug:
                r0 = row0 + tb * 128
                P.add("pool", lambda e, r0=r0, tb=tb: e.dma_start(out=dbg2_d[r0:r0 + 128, 0:512], in_=ug3[:, tb, :]),
                      reads=[R("ug", tb * 2048, (tb + 1) * 2048)], writes=[R("dbg2", r0, r0 + 128)], chan=ch_dbg2[0])

        def gmlp_piece(hh):
            if hh < 4:
                tb = hh
                P.add("dve", lambda e, tb=tb: e.scalar_tensor_tensor(
                    out=aout[:, :], in0=ug3[:, tb, :], scalar=1.0, in1=ug3[:, tb, :], op0=ALU.mult, op1=ALU.mult,
                    accum_out=stc(ST_SSA + tb)),
                    reads=[R("ug", tb * 2048, (tb + 1) * 2048)], writes=[R("aout", 0, 2048), r_st(ST_SSA + tb)])
                if hh == 3:
                    rms_rstd(ST_SSA, 4, ST_RSA, 1.0 / 512, RMS_EPS)
            else:
                tb = hh - 4
                P.add("dve", lambda e, tb=tb: e.tensor_scalar(out=merged3[:, tb, 0:512], in0=ug3[:, tb, :],
                                                              scalar1=stc(ST_RSA + tb), scalar2=None, op0=ALU.mult),
                      reads=[R("ug", tb * 2048, (tb + 1) * 2048), r_st(ST_RSA + tb)],
                      writes=[R("merged", tb * 2048, tb * 2048 + 1024)])

        nkb = blk0 + 4
        sbanks = [2, 3, 4]
        scnt = [0]
        ecnt = [0]
        TB7 = 7
        O3 = ps[:, TB7, 0:260].rearrange("p (q c) -> p q c", q=4)
        tails = []

        def make_tail(h, ob):
            otp = mt[0:65, (h % 2) * 512:(h % 2 + 1) * 512]
            r_ot = R("mt", (h % 2) * 2048, (h % 2 + 1) * 2048)

            def tail():
                def pe_tr(e):
                    ins = None
                    for tb in range(4):
                        ins = e.transpose(out=O3[:, tb, :], in_=otp[:, tb * 128:(tb + 1) * 128], identity=identf[0:65, 0:65])
                    return ins
                P.add("pe", pe_tr, reads=[r_ot, R("cf", 1536, 2048)], writes=[r_ps(TB7)])
                rdc = ST_RDEN + 4 * (h % 2)
                P.add("dve", lambda e: e.reciprocal(out=stc(rdc, 4), in_=O3[:, :, 64]),
                      reads=[r_ps(TB7)], writes=[r_st(rdc, 4)])
                P.add("dve", lambda e: e.tensor_tensor(
                    out=bo3[:, :, h * 64:(h + 1) * 64], in0=O3[:, :, 0:64],
                    in1=stc(rdc, 4).unsqueeze(2).broadcast_to([128, 4, 64]), op=ALU.mult),
                    reads=[r_ps(TB7), r_st(rdc, 4)],
                    writes=[r_bo(tb, h * 64, (h + 1) * 64) for tb in range(4)])
            return tail

        for h in range(8):
            ob = 5 + (h % 2)
            OT = ps[0:65, ob, :]
            pend = []

            def emit_pv(item, h=h, ob=ob, OT=OT):
                j, r, es_ = item
                c0 = max(0, r) * 128

                def pe_pv(e, j=j, c0=c0, es_=es_):
                    return e.matmul(out=OT[:, c0:512], lhsT=VA4[:, j, h, :], rhs=ET[:, es_ * 512 + c0:(es_ + 1) * 512],
                                    start=(j == 0), stop=(j == nkb - 1), skip_group_check=True)
                P.add("pe", pe_pv, reads=[R("ET", (es_ * 512 + c0) * 2, (es_ + 1) * 1024), r_VA(j, h)],
                      writes=[r_ps(ob)])

            for j in range(nkb):
                r = j - blk0
                c0 = max(0, r) * 128
                sbk = sbanks[scnt[0] % 3]
                scnt[0] += 1
                es_ = ecnt[0] % 4
                ecnt[0] += 1

                def pe_qk(e, h=h, j=j, r=r, c0=c0, sbk=sbk):
                    ins = e.matmul(out=ps[:, sbk, c0:512], lhsT=KT3[0:70, h, j * 128:(j + 1) * 128],
                                   rhs=QT3[0:70, h, c0:512], start=True, stop=(r < 0), skip_group_check=True)
                    if r >= 0:
                        ins = e.matmul(out=ps[:, sbk, c0:c0 + 128], lhsT=ident, rhs=maskneg, start=False, stop=True,
                                       skip_group_check=True)
                    return ins
                P.add("pe", pe_qk, reads=[r_KT(h, j), r_QT(h), R("cb", 0, 512)], writes=[r_ps(sbk)])
                P.add("act", lambda e, c0=c0, sbk=sbk, es_=es_: e.activation(
                    out=ET[:, es_ * 512 + c0:(es_ + 1) * 512], in_=ps[:, sbk, c0:512], func=AF.Exp),
                    reads=[r_ps(sbk)], writes=[R("ET", (es_ * 512 + c0) * 2, (es_ + 1) * 1024)])
                pend.append((j, r, es_))
                if len(pend) > 2:
                    emit_pv(pend.pop(0))
                if j == 1 and tails:
                    tails.pop(0)()
            while pend:
                emit_pv(pend.pop(0))
            P.add("dve", lambda e, h=h, OT=OT: e.tensor_copy(out=mt[0:65, (h % 2) * 512:(h % 2 + 1) * 512], in_=OT),
                  reads=[r_ps(ob)], writes=[R("mt", (h % 2) * 2048, (h % 2 + 1) * 2048)])
            tails.append(make_tail(h, ob))
            gmlp_piece(h)
        while tails:
            tails.pop(0)()

        for tb in range(4):
            if tb < 2:
                P.add("act", lambda e, tb=tb: e.activation(out=junk[:, 0:512], in_=bo3[:, tb, :], func=AF.Square,
                                                           accum_out=stc(ST_SSB + tb)),
                      reads=[r_bo(tb)], writes=[R("junk", 0, 1024), r_st(ST_SSB + tb)])
            else:
                P.add("dve", lambda e, tb=tb: e.scalar_tensor_tensor(
                    out=aout[:, :], in0=bo3[:, tb, :], scalar=1.0, in1=bo3[:, tb, :], op0=ALU.mult, op1=ALU.mult,
                    accum_out=stc(ST_SSB + tb)),
                    reads=[r_bo(tb)], writes=[R("aout", 0, 2048), r_st(ST_SSB + tb)])
            if debug:
                r0 = row0 + tb * 128
                P.add("pool", lambda e, r0=r0, tb=tb: e.dma_start(out=dbg2_d[r0:r0 + 128, 512:1024], in_=bo3[:, tb, :]),
                      reads=[r_bo(tb)], writes=[R("dbg2", r0, r0 + 128)], chan=ch_dbg2[1])
        rms_rstd(ST_SSB, 4, ST_RSB, 1.0 / 512, RMS_EPS)
        for tb in range(4):
            if tb % 2 == 0:
                P.add("act", lambda e, tb=tb: e.activation(out=merged3[:, tb, 512:1024], in_=bo3[:, tb, :], func=AF.Copy,
                                                           scale=stc(ST_RSB + tb)),
                      reads=[r_bo(tb), r_st(ST_RSB + tb)], writes=[R("merged", tb * 2048 + 1024, (tb + 1) * 2048)])
            else:
                P.add("dve", lambda e, tb=tb: e.tensor_scalar(out=merged3[:, tb, 512:1024], in0=bo3[:, tb, :],
                                                              scalar1=stc(ST_RSB + tb), scalar2=None, op0=ALU.mult),
                      reads=[r_bo(tb), r_st(ST_RSB + tb)], writes=[R("merged", tb * 2048 + 1024, (tb + 1) * 2048)])
        for tb in range(4):
            transpose_block(lambda k, tb=tb: merged3[:, tb, k * 128:(k + 1) * 128],
                            [R("merged", tb * 2048, (tb + 1) * 2048)], tb, [0, 1, 4, 5][tb], G_GT_OUTN, r_actT_tb(tb))
        s0, sl0 = load_slot(s_wout3[:, :, 0:512], lambda a: a.rearrange("p (k c) -> p k c", k=8), [R("s_wout", 0, 512)])
        s1, sl1 = load_slot(s_wout3[:, :, 512:1024], lambda a: a.rearrange("p (k c) -> p k c", k=8), [R("s_wout", 512, 1024)])
        wsl = [sl0.rearrange("p (k c) -> p k c", k=8), sl1.rearrange("p (k c) -> p k c", k=8)]
        obanks = [(2, 3), (4, 5), (6, 7), (0, 1)]
        htbank = [2, 3, 4, 5]
        def emit_pe_o(tb):
            banks = obanks[tb]

            def pe_o(e, tb=tb, banks=banks):
                ins = None
                for ct in range(2):
                    for k in range(8):
                        ins = e.matmul(out=ps[:, banks[ct], :], lhsT=actT3[:, k, tb * 128:(tb + 1) * 128],
                                       rhs=wsl[ct][:, k, :], start=(k == 0), stop=(k == 7))
                return ins
            P.add("pe", pe_o, reads=r_actT_tb(tb) + [r_ring(s0), r_ring(s1)], writes=[r_ps(banks[0]), r_ps(banks[1])])

        def emit_evac_o(tb):
            banks = obanks[tb]
            b0 = banks[0]
            rb = [r_ps(banks[0]), r_ps(banks[1])]
            P.add("act", lambda e, tb=tb, b0=b0: e.activation(out=junk[:, :].rearrange("p (c d) -> p c d", c=2),
                                                              in_=ps[:, b0:b0 + 2, :], func=AF.Square,
                                                              accum_out=stc(ST_SSM + tb)),
                  reads=rb, writes=[R("junk", 0, 2048), r_st(ST_SSM + tb)])
            rms_rstd(ST_SSM + tb, 1, ST_RSM + tb, 1.0 / D, RMS_EPS)
            P.add("dve", lambda e, tb=tb, b0=b0: e.scalar_tensor_tensor(
                out=mt[:, :].rearrange("p (c d) -> p c d", c=2), in0=ps[:, b0:b0 + 2, :], scalar=stc(ST_RSM + tb),
                in1=gains[:, G_POSTMIX:G_POSTMIX + 1024].rearrange("p (c d) -> p c d", c=2), op0=ALU.mult, op1=ALU.mult),
                reads=rb + [r_st(ST_RSM + tb), R("gains", G_POSTMIX * 4, (G_POSTMIX + 1024) * 4)],
                writes=[R("mt", 0, 4096)])
            P.add("dve", lambda e, tb=tb, hp=hp: e.tensor_tensor(
                out=hg4[:, hp, tb, :], in0=hg4[:, hp, tb, :], in1=mt[:, :], op=ALU.add),
                reads=[R("mt", 0, 4096), r_hg(hp, tb)], writes=[r_hg(hp, tb)])
            if debug:
                r0 = row0 + tb * 128
                P.add("pool", lambda e, tb=tb, r0=r0, hp=hp: e.dma_start(out=dbg_d[r0:r0 + 128, :], in_=hg4[:, hp, tb, :]),
                      reads=[r_hg(hp, tb)], writes=[R("dbg", r0, r0 + 128)], chan=ch_dbg[tb])
        def emit_evac_o2(tb):
            P.add("act", lambda e, tb=tb, hp=hp: e.activation(out=junk[:, :], in_=hg4[:, hp, tb, :], func=AF.Square,
                                                              accum_out=stc(ST_SSH + tb)),
                  reads=[r_hg(hp, tb)], writes=[R("junk", 0, 2048), r_st(ST_SSH + tb)])
            rms_rstd(ST_SSH + tb, 1, ST_RSH + tb, 1.0 / D, RMS_EPS)
            par = tb % 2
            P.add("act", lambda e, tb=tb, par=par, hp=hp: e.activation(
                out=xn[:, par * D:(par + 1) * D], in_=hg4[:, hp, tb, :], func=AF.Copy, scale=stc(ST_RSH + tb)),
                reads=[r_hg(hp, tb), r_st(ST_RSH + tb)], writes=[R("xn", par * 2048, (par + 1) * 2048)])
            transpose_block(lambda k, par=par: xn[:, par * D + k * 128: par * D + (k + 1) * 128],
                            [R("xn", par * 2048, (par + 1) * 2048)], tb, htbank[tb], G_GT_FFN, r_actT_tb(tb))

        emit_pe_o(0)
        emit_pe_o(1)
        emit_pe_o(2)
        emit_pe_o(3)
        emit_evac_o(0)
        emit_evac_o(1)
        emit_evac_o2(0)
        emit_evac_o(2)
        emit_evac_o2(1)
        emit_evac_o(3)
        emit_evac_o2(2)
        emit_evac_o2(3)

        if g + 1 < ngroups:
            xload(g + 1)

        if g == 0:
            issue_convs(100)
        fcnt = [0]
        for sl_i in range(11):
            s, sl = load_slot(s_wfi4[:, :, :, sl_i * 256:(sl_i + 1) * 256],
                              lambda a: a.rearrange("p (k u c) -> p k u c", k=8, u=2),
                              [R("s_wfi", sl_i * 256, (sl_i + 1) * 256)], split_u=True)
            w4 = sl.rearrange("p (k u c) -> p k u c", k=8, u=2)
            for fi in range(2):
                f = sl_i * 2 + fi
                par = fcnt[0] % 2
                fcnt[0] += 1
                bg, bu = (2, 3) if par == 0 else (4, 5)

                def pe_f2(e, fi=fi, bg=bg, bu=bu, w4=w4):
                    ins = None
                    for k in range(8):
                        ins = e.matmul(out=ps[:, bg, :], lhsT=w4[:, k, 0, fi * 128:(fi + 1) * 128], rhs=actT3[:, k, :],
                                       start=(k == 0), stop=(k == 7))
                    for k in range(8):
                        ins = e.matmul(out=ps[:, bu, :], lhsT=w4[:, k, 1, fi * 128:(fi + 1) * 128], rhs=actT3[:, k, :],
                                       start=(k == 0), stop=(k == 7))
                    return ins
                P.add("pe", pe_f2, reads=r_actT_all() + [r_ring(s)], writes=[r_ps(bg), r_ps(bu)])
                sgp = sg[:, par * 512:(par + 1) * 512]
                rsg = R("mt", par * 2048, (par + 1) * 2048)
                P.add("act", lambda e, bg=bg, sgp=sgp: e.activation(out=sgp, in_=ps[:, bg, :], func=AF.Silu),
                      reads=[r_ps(bg)], writes=[rsg])
                P.add("dve", lambda e, f=f, bu=bu, sgp=sgp: e.tensor_tensor(out=ffT3[:, f, :], in0=sgp, in1=ps[:, bu, :],
                                                                            op=ALU.mult),
                      reads=[rsg, r_ps(bu)], writes=[r_ffT(f)])

    G_BANKS = {0: (2, 3), 1: (4, 5), 2: (6, 7), 3: (0, 1)}

    def stageG_pe(g, pss):
        tbs = (0, 1) if pss == 0 else (2, 3)
        for sl_i in range(6):
            f0, f1 = sl_i * 4, min(NFT, sl_i * 4 + 4)
            s, sl = load_slot(s_wfo3[:, f0:f1, :], lambda a, n=f1 - f0: a[:, 0:n * 1024].rearrange("p (k c) -> p k c", k=n),
                              [R("s_wfo", f0, f1)])
            w3 = sl[:, 0:(f1 - f0) * 1024].rearrange("p (k c) -> p k c", k=f1 - f0)

            def pe_fo(e, f0=f0, f1=f1, w3=w3, tbs=tbs):
                ins = None
                for tb in tbs:
                    for ct in range(2):
                        for f in range(f0, f1):
                            ins = e.matmul(out=ps[:, G_BANKS[tb][ct], :], lhsT=ffT3[:, f, tb * 128:(tb + 1) * 128],
                                           rhs=w3[:, f - f0, ct * 512:(ct + 1) * 512],
                                           start=(f == 0), stop=(f == NFT - 1))
                return ins
            P.add("pe", pe_fo, reads=[r_ffT(f) for f in range(f0, f1)] + [r_ring(s)],
                  writes=[r_ps(b) for tb in tbs for b in G_BANKS[tb]])

    def stageG_evac(g, pss):
        hp = g % 2
        tbs = (0, 1) if pss == 0 else (2, 3)
        for tb in tbs:
            b0 = G_BANKS[tb][0]
            rb = [r_ps(b0), r_ps(b0 + 1)]
            P.add("act", lambda e, tb=tb, b0=b0: e.activation(out=junk[:, :].rearrange("p (c d) -> p c d", c=2),
                                                              in_=ps[:, b0:b0 + 2, :], func=AF.Square,
                                                              accum_out=stc(ST_SSO + tb)),
                  reads=rb, writes=[R("junk", 0, 2048), r_st(ST_SSO + tb)])
            rms_rstd(ST_SSO + tb, 1, ST_RSO + tb, 1.0 / D, RMS_EPS)
            P.add("dve", lambda e, tb=tb, b0=b0: e.scalar_tensor_tensor(
                out=mt[:, :].rearrange("p (c d) -> p c d", c=2), in0=ps[:, b0:b0 + 2, :], scalar=stc(ST_RSO + tb),
                in1=gains[:, G_POSTFFN:G_POSTFFN + 1024].rearrange("p (c d) -> p c d", c=2), op0=ALU.mult, op1=ALU.mult),
                reads=rb + [r_st(ST_RSO + tb), R("gains", G_POSTFFN * 4, (G_POSTFFN + 1024) * 4)],
                writes=[R("mt", 0, 4096)])
            P.add("dve", lambda e, tb=tb, hp=hp: e.tensor_tensor(
                out=hg4[:, hp, tb, :], in0=hg4[:, hp, tb, :], in1=mt[:, :], op=ALU.add),
                reads=[R("mt", 0, 4096), r_hg(hp, tb)], writes=[r_hg(hp, tb)])
            r0 = g * 512 + tb * 128
            P.add("pool", lambda e, tb=tb, r0=r0, hp=hp: e.dma_start(out=out_d[r0:r0 + 128, :], in_=hg4[:, hp, tb, :]),
                  reads=[r_hg(hp, tb)], writes=[R("out", r0, r0 + 128)], chan=ch_o[hp * 4 + tb])

    xload(0)
    late_consts()
    stageA(0, (2, 3, 4, 5))
    conv_win_rest()
    late_setup()
    for g in range(ngroups):
        stagesBF(g)
        if upto != "G":
            continue
        stageG_pe(g, 0)
        stageG_pe(g, 1)
        stageG_evac(g, 0)
        if g + 1 < ngroups:
            stageA(g + 1, (2, 3, 4, 5))
        stageG_evac(g, 1)

    P.add("pool", None, reads=[R("out", 0, TOK)] + ([R("dbg", 0, TOK), R("dbg2", 0, TOK)] if debug else []))

    eng_sems = {e: sem("es_" + e) for e in Prog.ENGS}
    P.finalize(eng_sems)
    with nc.Block() as block:
        @block.tensor
        def _(e):
            P.run_engine(e, "pe")

        @block.scalar
        def _(e):
            P.run_engine(e, "act")

        @block.vector
        def _(e):
            P.run_engine(e, "dve")

        @block.gpsimd
        def _(e):
            P.run_engine(e, "pool")

        @block.sync
        def _(e):
            P.run_engine(e, "sp")
    es.close()
    return nc


def _host_consts(pre_mix_gain, ln_v_gain, ln_v_bias, w_spatial, b_spatial, b_forget, out_norm_a_gain,
                 out_norm_b_gain, post_mix_gain, pre_ffn_gain, post_ffn_gain):
    f = np.float32
    gains = np.zeros((128, GW), f)
    gains[:, G_POSTMIX:G_POSTMIX + 1024] = np.asarray(post_mix_gain, f).reshape(1, 1024)
    gains[:, G_POSTFFN:G_POSTFFN + 1024] = np.asarray(post_ffn_gain, f).reshape(1, 1024)
    gains[:, G_LNVG:G_LNVG + 512] = np.asarray(ln_v_gain, f).reshape(1, 512)
    gains[:, G_LNVB:G_LNVB + 512] = np.asarray(ln_v_bias, f).reshape(1, 512)
    bs = np.asarray(b_spatial, f).reshape(8, 128)
    gains[:, G_BS:G_BS + 512] = np.repeat(bs.T[:, :, None], 64, axis=2).reshape(128, 512)
    gains[:, G_BFOR:G_BFOR + 8] = np.asarray(b_forget, f).reshape(1, 8)
    gains[:, G_GT_PRE:G_GT_PRE + 8] = np.asarray(pre_mix_gain, f).reshape(8, 128).T
    outn = np.concatenate([np.asarray(out_norm_a_gain, f).reshape(-1), np.asarray(out_norm_b_gain, f).reshape(-1)])
    gains[:, G_GT_OUTN:G_GT_OUTN + 8] = outn.reshape(8, 128).T
    gains[:, G_GT_FFN:G_GT_FFN + 8] = np.asarray(pre_ffn_gain, f).reshape(8, 128).T
    gains[:, G_NEGH] = -0.5
    s = np.arange(128)[:, None]
    t = np.arange(128)[None, :]
    cf = np.zeros((128, 512), f)
    cf[:, 384:512] = np.eye(128, dtype=f)
    cf[:, 0:128] = (t >= s).astype(f)
    cf[:, 128:256] = -(s <= t).astype(f)
    cf[:, 256:384] = -1.0
    cb = np.zeros((128, 256), f)
    cb[:, 0:128] = np.eye(128, dtype=f)
    cb[:, 128:256] = np.where(s > t, -30000.0, 0.0)
    cb = cb.astype(ml_dtypes.bfloat16)
    ws = np.asarray(w_spatial, f).reshape(8, 128, 128)
    wsT = np.ascontiguousarray(ws.transpose(2, 0, 1)).reshape(128, 1024)
    return gains, cf, cb, wsT


_NC_CACHE = {}


def kernel(x, pre_mix_gain, w_in, ln_v_gain, ln_v_bias, w_spatial, b_spatial, b_forget, out_norm_a_gain,
           out_norm_b_gain, w_out, post_mix_gain, pre_ffn_gain, w_ffn_in, w_ffn_out, post_ffn_gain):
    x = np.asarray(x, np.float32)
    gains, cf, cb, wsT = _host_consts(pre_mix_gain, ln_v_gain, ln_v_bias, w_spatial, b_spatial, b_forget,
                                      out_norm_a_gain, out_norm_b_gain, post_mix_gain, pre_ffn_gain, post_ffn_gain)
    common = {
        "w_in": np.ascontiguousarray(np.asarray(w_in, np.float32).reshape(D, INC)),
        "w_out": np.ascontiguousarray(np.asarray(w_out, np.float32).reshape(D, D)),
        "w_ffn_in": np.ascontiguousarray(np.asarray(w_ffn_in, np.float32).reshape(D, 2 * DFF)),
        "w_ffn_out": np.ascontiguousarray(np.asarray(w_ffn_out, np.float32).reshape(DFF, D)),
        "gains": gains, "cf": cf, "cb": cb, "wsT": wsT,
    }
    xs = x.reshape(NCORES, TOK, D)
    in_maps = [dict(common, x=np.ascontiguousarray(xs[c])) for c in range(NCORES)]
    if "nc" not in _NC_CACHE:
        _NC_CACHE["nc"] = build_nc()
    res = run_bass_kernel_spmd(_NC_CACHE["nc"], in_maps, core_ids=list(range(NCORES)))
    out = np.stack([np.asarray(r["out"]) for r in res.results], axis=0)
    return out.reshape(16, SEQ, D).astype(np.float32)
```
